# Optimizing a Trainium2 kernel written in Bass

```python
import math
import jax, jax.numpy as jnp
from jax import lax
import numpy as np

D_MODEL = 1024
BATCH = 4
SEQ = 4096
DEPTH = 2

CHUNK = 64
Q_BLOCK = 128
D_MIX = D_MODEL
SSD_WIDTH = D_MIX // 2
SSD_HEAD_DIM = 64
SSD_HEADS = SSD_WIDTH // SSD_HEAD_DIM
SSD_GROUPS = 2
SSD_HPG = SSD_HEADS // SSD_GROUPS
SSD_STATE = 128
SSD_CONV = 4
SSD_CONV_DIM = SSD_WIDTH + 2 * SSD_GROUPS * SSD_STATE
SSD_IN = SSD_WIDTH + SSD_CONV_DIM + SSD_HEADS
FOX_WIDTH = D_MIX // 4
FOX_HEAD_DIM = 64
FOX_HEADS = FOX_WIDTH // FOX_HEAD_DIM
FOX_IN = 3 * FOX_WIDTH + FOX_HEADS
SCONV_WIDTH = D_MIX - SSD_WIDTH - FOX_WIDTH
SCONV_K = 3
SCONV_IN = 3 * SCONV_WIDTH
D_IN_PROJ = SSD_IN + FOX_IN + SCONV_IN
D_FF = 2816
ALPHA = (2 * DEPTH) ** 0.25
BETA = (8 * DEPTH) ** -0.25
LN_EPS = 1e-5
RMS_EPS = 1e-5
N_SUB = 3

kernel_name = "hybrid_ssd_fox_shortconv_macaron_deepnorm_adaln"


def layer_norm(x, g, b):
    xf = x.astype(jnp.float32)
    mu = jnp.mean(xf, axis=-1, keepdims=True)
    var = jnp.mean(jnp.square(xf - mu), axis=-1, keepdims=True)
    return ((xf - mu) * lax.rsqrt(var + LN_EPS) * g + b).astype(x.dtype)


def causal_depthwise_conv(x, w, b=None):
    k_w, ch = w.shape
    y = lax.conv_general_dilated(
        x, w[:, None, :], window_strides=(1,), padding=[(k_w - 1, 0)],
        dimension_numbers=("NWC", "WIO", "NWC"), feature_group_count=ch)
    return y if b is None else y + b


def swiglu_ffn(h, w_in, w_out):
    gate, up = jnp.split(h @ w_in, 2, axis=-1)
    return (jax.nn.silu(gate) * up) @ w_out


def segsum(a):
    t = a.shape[-1]
    cs = jnp.cumsum(a, axis=-1)
    diff = cs[..., :, None] - cs[..., None, :]
    mask = jnp.tril(jnp.ones((t, t), dtype=bool))
    return jnp.where(mask, diff, -jnp.inf)


def ssd_chunked_scan(xdt, a, bm, cm):
    b, seq, g, e, p = xdt.shape
    n = bm.shape[-1]
    nc = seq // CHUNK
    xdt = xdt.reshape(b, nc, CHUNK, g, e, p)
    bm = bm.reshape(b, nc, CHUNK, g, n)
    cm = cm.reshape(b, nc, CHUNK, g, n)
    a = jnp.transpose(a.reshape(b, nc, CHUNK, g, e), (0, 3, 4, 1, 2))
    a_cs = jnp.cumsum(a, axis=-1)
    l_mat = jnp.exp(segsum(a))
    cb = jnp.einsum("bclgn,bcsgn->bgcls", cm, bm)
    y_diag = jnp.einsum("bgcls,bgecls,bcsgep->bclgep", cb, l_mat, xdt)
    decay_states = jnp.exp(a_cs[..., -1:] - a_cs)
    states = jnp.einsum("bclgn,bgecl,bclgep->bcgepn", bm, decay_states, xdt)
    chunk_a = jnp.pad(a_cs[..., -1], ((0, 0), (0, 0), (0, 0), (1, 0)))
    decay_chunk = jnp.exp(segsum(chunk_a))
    states = jnp.concatenate([jnp.zeros_like(states[:, :1]), states], axis=1)
    states_in = jnp.einsum("bgezc,bcgepn->bzgepn", decay_chunk, states)[:, :-1]
    y_off = jnp.einsum("bclgn,bcgepn,bgecl->bclgep", cm, states_in, jnp.exp(a_cs))
    return (y_diag + y_off).reshape(b, seq, g, e, p)


def mamba2_group(proj, conv_w, conv_b, dt_bias, a_log, d_skip, norm_g):
    b, seq, _ = proj.shape
    z, xbc, dt_raw = jnp.split(proj, [SSD_WIDTH, SSD_WIDTH + SSD_CONV_DIM], axis=-1)
    xbc = jax.nn.silu(causal_depthwise_conv(xbc, conv_w, conv_b))
    xs, bm, cm = jnp.split(xbc, [SSD_WIDTH, SSD_WIDTH + SSD_GROUPS * SSD_STATE], axis=-1)
    xs = xs.astype(jnp.float32).reshape(b, seq, SSD_GROUPS, SSD_HPG, SSD_HEAD_DIM)
    bm = bm.astype(jnp.float32).reshape(b, seq, SSD_GROUPS, SSD_STATE)
    cm = cm.astype(jnp.float32).reshape(b, seq, SSD_GROUPS, SSD_STATE)
    dt = jax.nn.softplus(dt_raw.astype(jnp.float32) + dt_bias.astype(jnp.float32))
    dt = dt.reshape(b, seq, SSD_GROUPS, SSD_HPG)
    a_head = -jnp.exp(a_log.astype(jnp.float32)).reshape(SSD_GROUPS, SSD_HPG)
    y = ssd_chunked_scan(xs * dt[..., None], dt * a_head, bm, cm)
    y = y + d_skip.astype(jnp.float32).reshape(SSD_GROUPS, SSD_HPG)[:, :, None] * xs
    y = y.reshape(b, seq, SSD_WIDTH) * jax.nn.silu(z.astype(jnp.float32))
    yg = y.reshape(b, seq, SSD_GROUPS, SSD_WIDTH // SSD_GROUPS)
    yg = yg * lax.rsqrt(jnp.mean(jnp.square(yg), axis=-1, keepdims=True) + RMS_EPS)
    return (yg.reshape(b, seq, SSD_WIDTH) * norm_g).astype(proj.dtype)


def fox_group(proj, f_bias):
    b, seq, _ = proj.shape
    q, k, v, f = jnp.split(proj, [FOX_WIDTH, 2 * FOX_WIDTH, 3 * FOX_WIDTH], axis=-1)
    q = q.reshape(b, seq, FOX_HEADS, FOX_HEAD_DIM)
    k = k.reshape(b, seq, FOX_HEADS, FOX_HEAD_DIM)
    v = v.reshape(b, seq, FOX_HEADS, FOX_HEAD_DIM)
    log_f = jax.nn.log_sigmoid(f.astype(jnp.float32) + f_bias.astype(jnp.float32))
    cum_f = jnp.transpose(jnp.cumsum(log_f, axis=1), (0, 2, 1))
    scale = FOX_HEAD_DIM ** -0.5
    outs = []
    for i in range(seq // Q_BLOCK):
        q0, end = i * Q_BLOCK, (i + 1) * Q_BLOCK
        s = jnp.einsum("bqhd,bkhd->bhqk", q[:, q0:end], k[:, :end]).astype(jnp.float32) * scale
        s = s + cum_f[:, :, q0:end, None] - cum_f[:, :, None, :end]
        mask = jnp.arange(q0, end)[:, None] >= jnp.arange(end)[None, :]
        prob = jax.nn.softmax(jnp.where(mask, s, -jnp.inf), axis=-1)
        outs.append(jnp.einsum("bhqk,bkhd->bqhd", prob.astype(v.dtype), v[:, :end]))
    return jnp.concatenate(outs, axis=1).reshape(b, seq, FOX_WIDTH)


def shortconv_group(proj, conv_w):
    bg, cg, xin = jnp.split(proj, 3, axis=-1)
    return bg * causal_depthwise_conv(cg * xin, conv_w)


def modulate(x, shift, scale):
    return x * (1.0 + scale) + shift


def setup_inputs(seed: int = 0) -> dict:
    key = jax.random.key(seed)
    ks = jax.random.split(key, 24)

    def nrm(k, shape, s):
        return jax.random.normal(k, shape, jnp.float32) * s

    dt0 = jnp.exp(jax.random.uniform(ks[12], (DEPTH, SSD_HEADS), jnp.float32,
                                     minval=math.log(1e-3), maxval=math.log(1e-1)))
    return {
        "x": nrm(ks[0], (BATCH, SEQ, D_MODEL), 1.0),
        "c": nrm(ks[1], (BATCH, D_MODEL), 1.0),
        "ln_in_g": 1.0 + nrm(ks[2], (D_MODEL,), 0.01),
        "ln_in_b": nrm(ks[3], (D_MODEL,), 0.01),
        "ada_w": nrm(ks[4], (DEPTH, D_MODEL, N_SUB * 3 * D_MODEL), 0.5 * D_MODEL ** -0.5),
        "ada_b": nrm(ks[5], (DEPTH, N_SUB * 3 * D_MODEL), 0.01),
        "ffn1_w_in": nrm(ks[6], (DEPTH, D_MODEL, 2 * D_FF), D_MODEL ** -0.5),
        "ffn1_w_out": nrm(ks[7], (DEPTH, D_FF, D_MODEL), BETA * D_FF ** -0.5),
        "mix_w_in": nrm(ks[8], (DEPTH, D_MODEL, D_IN_PROJ), D_MODEL ** -0.5),
        "mix_w_out": nrm(ks[9], (DEPTH, D_MIX, D_MODEL), BETA * D_MIX ** -0.5),
        "ssd_conv_w": nrm(ks[10], (DEPTH, SSD_CONV, SSD_CONV_DIM), SSD_CONV ** -0.5),
        "ssd_conv_b": nrm(ks[11], (DEPTH, SSD_CONV_DIM), 0.01),
        "ssd_dt_bias": dt0 + jnp.log(-jnp.expm1(-dt0)),
        "ssd_a_log": jnp.log(jax.random.uniform(ks[13], (DEPTH, SSD_HEADS), jnp.float32,
                                                 minval=1.0, maxval=16.0)),
        "ssd_d": 1.0 + nrm(ks[14], (DEPTH, SSD_HEADS), 0.01),
        "ssd_norm_g": 1.0 + nrm(ks[15], (DEPTH, SSD_WIDTH), 0.01),
        "fox_f_bias": jax.random.uniform(ks[16], (DEPTH, FOX_HEADS), jnp.float32,
                                         minval=1.0, maxval=5.0),
        "sconv_w": nrm(ks[17], (DEPTH, SCONV_K, SCONV_WIDTH), SCONV_K ** -0.5),
        "ffn2_w_in": nrm(ks[18], (DEPTH, D_MODEL, 2 * D_FF), D_MODEL ** -0.5),
        "ffn2_w_out": nrm(ks[19], (DEPTH, D_FF, D_MODEL), BETA * D_FF ** -0.5),
        "ln_g": 1.0 + nrm(ks[20], (DEPTH, N_SUB, D_MODEL), 0.01),
        "ln_b": nrm(ks[21], (DEPTH, N_SUB, D_MODEL), 0.01),
    }


def reference(x, c, ln_in_g, ln_in_b, ada_w, ada_b, ffn1_w_in, ffn1_w_out,
              mix_w_in, mix_w_out, ssd_conv_w, ssd_conv_b, ssd_dt_bias, ssd_a_log,
              ssd_d, ssd_norm_g, fox_f_bias, sconv_w, ffn2_w_in, ffn2_w_out,
              ln_g, ln_b):
    b = x.shape[0]
    x = layer_norm(x, ln_in_g, ln_in_b)
    c_act = jax.nn.silu(c)
    for l in range(DEPTH):
        mod = (c_act @ ada_w[l] + ada_b[l]).reshape(b, N_SUB, 3, 1, D_MODEL)

        h = modulate(x, mod[:, 0, 0], mod[:, 0, 1])
        y = swiglu_ffn(h, ffn1_w_in[l], ffn1_w_out[l])
        x = layer_norm(ALPHA * x + 0.5 * mod[:, 0, 2] * y, ln_g[l, 0], ln_b[l, 0])

        h = modulate(x, mod[:, 1, 0], mod[:, 1, 1])
        proj = h @ mix_w_in[l]
        p_ssd, p_fox, p_sc = jnp.split(proj, [SSD_IN, SSD_IN + FOX_IN], axis=-1)
        y_ssd = mamba2_group(p_ssd, ssd_conv_w[l], ssd_conv_b[l], ssd_dt_bias[l],
                             ssd_a_log[l], ssd_d[l], ssd_norm_g[l])
        y_fox = fox_group(p_fox, fox_f_bias[l])
        y_sc = shortconv_group(p_sc, sconv_w[l])
        y = jnp.concatenate([y_ssd, y_fox, y_sc], axis=-1) @ mix_w_out[l]
        x = layer_norm(ALPHA * x + mod[:, 1, 2] * y, ln_g[l, 1], ln_b[l, 1])

        h = modulate(x, mod[:, 2, 0], mod[:, 2, 1])
        y = swiglu_ffn(h, ffn2_w_in[l], ffn2_w_out[l])
        x = layer_norm(ALPHA * x + 0.5 * mod[:, 2, 2] * y, ln_g[l, 2], ln_b[l, 2])
    return x
```

```python
import numpy as np
from contextlib import ExitStack
import concourse.bass as bass
import concourse.mybir as mybir
from concourse.bass_utils import run_bass_kernel_spmd

F32 = mybir.dt.float32
F32R = mybir.dt.float32r
BF16 = mybir.dt.bfloat16
AF = mybir.ActivationFunctionType
ALU = mybir.AluOpType

D = 1024
SEQ = 4096
T = 1024
NSEG = SEQ // T
DFF = 2816
NF = DFF // 128
DEPTH = 2
ALPHA = (2 * DEPTH) ** 0.25
LN_EPS = 1e-5
RMS_EPS = 1e-5
NCORES = 4
NEG = -1.0e5


class Op:
    __slots__ = ("eng", "fn", "deps", "dma", "sig", "val", "idx", "force")


class Sched:
    def __init__(self, nc, es):
        self.nc = nc
        self.es = es
        self.h = {"pe": nc.tensor, "act": nc.scalar, "dve": nc.vector, "pool": nc.gpsimd, "sp": nc.sync}
        self.ops = []
        self.last_w = {}
        self.readers = {}
        self.streams = {}

    def op(self, eng, fn, reads=(), writes=(), dma=None):
        o = Op()
        o.eng, o.fn, o.dma, o.sig, o.val = eng, fn, dma, False, 0
        o.idx = len(self.ops)
        o.force = False
        deps = set()
        for r in reads:
            w = self.last_w.get(r)
            if w is not None:
                deps.add(w)
        for r in writes:
            w = self.last_w.get(r)
            if w is not None:
                deps.add(w)
            for x in self.readers.get(r, ()):
                deps.add(x)
        o.deps = deps
        for r in writes:
            self.last_w[r] = o.idx
            self.readers[r] = []
        for r in reads:
            self.readers.setdefault(r, []).append(o.idx)
        if dma is not None:
            assert self.streams.setdefault(dma, eng) == eng
        self.ops.append(o)
        return o

    def barrier_res(self):
        return list(self.last_w.keys() | self.readers.keys())

    def emit(self):
        nc = self.nc
        ops = self.ops
        for o in ops:
            for d in o.deps:
                ops[d].sig = True
        for o in ops:
            if o.dma is not None:
                o.sig = True
        sems = {}
        for e in self.h:
            sems[e] = self.es.enter_context(nc.semaphore("s_" + e))
        ssems = {}
        for s in self.streams:
            ssems[s] = self.es.enter_context(nc.semaphore("d_" + s))
        cnt = {e: 0 for e in self.h}
        scnt = {s: 0 for s in self.streams}
        seen = {e: {} for e in self.h}
        nwait = 0
        for o in ops:
            h = self.h[o.eng]
            need = {}
            for d in o.deps:
                do = ops[d]
                if do.dma is not None:
                    key = ("s", do.dma)
                    v = do.val
                    if self.streams[do.dma] == o.eng:
                        v = scnt[do.dma]
                else:
                    if do.eng == o.eng and o.eng in ("pe",) and o.dma is None and not o.force:
                        continue
                    key = ("e", do.eng)
                    v = do.val
                if need.get(key, 0) < v:
                    need[key] = v
            if o.dma is not None and scnt[o.dma] > 0:
                need[("s", o.dma)] = max(need.get(("s", o.dma), 0), scnt[o.dma])
            for key, v in need.items():
                if seen[o.eng].get(key, 0) >= v:
                    continue
                seen[o.eng][key] = v
                sem = ssems[key[1]] if key[0] == "s" else sems[key[1]]
                h.wait_ge(sem, v)
                nwait += 1
            ins = o.fn(h)
            if o.dma is not None:
                scnt[o.dma] += 16
                o.val = scnt[o.dma]
                ins.then_inc(ssems[o.dma], 16)
            elif o.sig:
                cnt[o.eng] += 1
                o.val = cnt[o.eng]
                ins.then_inc(sems[o.eng], 1)
        sp = self.h["sp"]
        for s, c in scnt.items():
            if c:
                sp.wait_ge(ssems[s], c)
        for e, c in cnt.items():
            if c and e != "sp":
                sp.wait_ge(sems[e], c)
        return dict(n_ops=len(ops), n_wait=nwait, cnt=cnt)


def _fm(v):
    v = np.asarray(v, np.float32)
    return np.ascontiguousarray(v.reshape(-1, 128).T)


def _kmaj(w):
    return np.ascontiguousarray(w.reshape(8, 128, w.shape[1]).transpose(1, 0, 2))


def _consts():
    k = np.arange(128)
    c = {}
    c["identF"] = np.eye(128, dtype=np.float32)
    c["U"] = (k[:, None] <= k[None, :]).astype(np.float32)
    c["Ms"] = (k[:, None] > k[None, :]).astype(np.float32)
    c["onesF"] = np.ones((128, 128), np.float32)
    c["maskT"] = (k[None, :] >= k[:, None]).astype(np.float32)
    tl = np.arange(512)
    m = np.zeros((128, 4, 512), np.float32)
    for r in range(4):
        m[:, r, :] = np.where(tl[None, :] < r * 128 + k[:, None], NEG, 0.0)
    c["masks"] = m.reshape(128, 2048)
    return np.ascontiguousarray(np.concatenate(
        [c["identF"], c["U"], c["Ms"], c["onesF"], c["maskT"], c["masks"]], axis=1))


C_ID, C_U, C_MS, C_ONE, C_MT, C_MASK = 0, 128, 256, 384, 512, 640
NCONST = 640 + 2048


def build(stage="full", nseg=NSEG, depth=DEPTH, dumps=()):
    nc = bass.Bass("TRN2", target_bir_lowering=False)
    es = ExitStack()
    S = Sched(nc, es)
    dumps = set(dumps)
    dump_specs = {}

    SHAPES = dict(
        xT=[NSEG, 128, 8, T], c_r=[128, 8], consts=[128, NCONST], lnG=[128, 56], lnB=[128, 56],
        adaw=[DEPTH, 18, 128, 8, 512], adab=[DEPTH, 1, 9216],
        fwin=[DEPTH, 2, NF, 128, 8, 256], fwout=[DEPTH, 2, 8, 128, NF, 128],
        wxbc=[DEPTH, 8, 128, 8, 128], wqk=[DEPTH, 8, 128, 8, 64], wsc=[DEPTH, 6, 128, 8, 128],
        wf=[DEPTH, 128, 8, 4], wz=[DEPTH, 128, 8, 512], wdt=[DEPTH, 128, 8, 8], wv=[DEPTH, 128, 8, 256],
        wmo=[DEPTH, 8, 128, 8, 128], cw=[DEPTH, 128, 32], cb=[DEPTH, 128, 8], ssdp=[DEPTH, 128, 24],
        normg=[DEPTH, 128, 512], fb=[DEPTH, 4, 1], scw=[DEPTH, 128, 6])
    declared = {}

    class _Lazy:
        def __init__(self, name):
            self.name = name

        def ap(self):
            if self.name not in declared:
                declared[self.name] = nc.dram_tensor(self.name, list(SHAPES[self.name]), F32, kind="ExternalInput").ap()
            return declared[self.name]

        def __getitem__(self, key):
            return self.ap()[key]

    xT_d, c_d, consts_d, lnG_d, lnB_d = _Lazy("xT"), _Lazy("c_r"), _Lazy("consts"), _Lazy("lnG"), _Lazy("lnB")
    adaw_d, adab_d, fwin_d, fwout_d = _Lazy("adaw"), _Lazy("adab"), _Lazy("fwin"), _Lazy("fwout")
    wxbc_d, wqk_d, wsc_d, wf_d, wz_d, wdt_d, wv_d, wmo_d = (_Lazy(n_) for n_ in ("wxbc", "wqk", "wsc", "wf", "wz", "wdt", "wv", "wmo"))
    cw_d, cb_d, ssdp_d, normg_d, fb_d, scw_d = (_Lazy(n_) for n_ in ("cw", "cb", "ssdp", "normg", "fb", "scw"))
    out_d = nc.dram_tensor("outT", [NSEG, 128, 8, T], F32, kind="ExternalOutput").ap()
    kh_d = nc.dram_tensor("khist", [DEPTH, NSEG, 4, 65, T], BF16).ap()
    vh_d = nc.dram_tensor("vhist", [DEPTH, NSEG, 4, 128, 8, 64], BF16).ap()

    uniq = [0]

    def sb(name, shape, dt=F32, ctx=None):
        uniq[0] += 1
        return (ctx or es).enter_context(nc.sbuf_tensor("s%d_%s" % (uniq[0], name), list(shape), dt))

    def dump(name, ap, shape, dt=F32, res=()):
        if name not in dumps:
            return
        t = nc.dram_tensor("dbg_" + name, list(shape), dt, kind="ExternalOutput").ap()
        S.op("sp", lambda e: e.dma_start(out=t, in_=ap), reads=res, writes=("dbg_" + name,), dma="dbg")
        dump_specs[name] = (shape, dt)

    xT = sb("xT", [128, 8, T])
    hT = sb("hT", [128, 8, T], BF16)
    cst = sb("cst", [128, NCONST])
    identB = sb("identB", [128, 128], BF16)
    masksB = sb("masksB", [128, 4, 512], BF16)
    lnG = sb("lnG", [128, 7, 8])
    lnB = sb("lnB", [128, 7, 8])
    mod = sb("mod", [128, DEPTH, 72])
    agT = sb("agT", [128, 7, 8])
    abT = sb("abT", [128, 7, 8])
    hsT = sb("hsT", [128, 6, 8])
    hbT = sb("hbT", [128, 6, 8])
    gsT = sb("gsT", [128, 6, 8])
    cact = sb("cact", [128, 8])
    cw = sb("cw", [128, DEPTH, 8, 4])
    cbias = sb("cbias", [128, DEPTH, 8])
    ssdp = sb("ssdp", [128, DEPTH, 24])
    Abc = sb("Abc", [128, DEPTH, 8])
    normg = sb("normg", [128, DEPTH, 512])
    nfb = sb("nfb", [4, DEPTH, 1])
    scw = sb("scw", [128, DEPTH, 2, 3])
    ctail = sb("ctail", [128, DEPTH, 8, 3])
    stail = sb("stail", [128, DEPTH, 2, 2])
    fcarry = sb("fcarry", [4, DEPTH, 1])
    kbias = sb("kbias", [128, DEPTH, 32, 4])
    Sst = sb("Sst", [128, DEPTH, 512])
    Sbf = sb("Sbf", [128, DEPTH, 512], BF16)
    ones4 = sb("ones4", [4, T])

    PB = [es.enter_context(nc.psum_tensor("pb%d" % i, [128, 512], F32)) for i in range(7)]
    PT = es.enter_context(nc.psum_tensor("pt", [128, 1024], BF16))

    identF = cst[:, C_ID:C_ID + 128]
    Umat = cst[:, C_U:C_U + 128]
    Msm = cst[:, C_MS:C_MS + 128]
    onesF = cst[:, C_ONE:C_ONE + 128]
    maskT = cst[:, C_MT:C_MT + 128]

    def barrier():
        allres = tuple(S.barrier_res())
        for eng in ("pe", "act", "dve", "pool", "sp"):
            o = S.op(eng, lambda e: e.nop(), reads=(), writes=allres)
            o.force = True

    ldc = [0]

    def ld(dst, src, res, eng="sp"):
        ldc[0] += 1
        S.op(eng, lambda e: e.dma_start(out=dst, in_=src), writes=(res,), dma="setup_%s%d" % (eng, ldc[0] % 6))

    ld(cst[:], consts_d[:, :], "cst")
    ld(lnG[:].rearrange("p a b -> p (a b)"), lnG_d[:, :], "lnG")
    ld(lnB[:].rearrange("p a b -> p (a b)"), lnB_d[:, :], "lnB")
    ld(cact[:], c_d[:, :], "cact")
    for l in range(DEPTH):
        ld(cw[:, l].rearrange("p a b -> p (a b)"), cw_d[l], "cw")
        ld(cbias[:, l], cb_d[l], "cbias")
        ld(ssdp[:, l], ssdp_d[l], "ssdp")
        ld(normg[:, l], normg_d[l], "normg")
        ld(nfb[:, l], fb_d[l], "nfb")
        ld(scw[:, l].rearrange("p a b -> p (a b)"), scw_d[l], "scw")
    ld(identB[:], consts_d[:, C_ID:C_ID + 128], "identB", eng="pool")
    ld(masksB[:].rearrange("p a b -> p (a b)"), consts_d[:, C_MASK:C_MASK + 2048], "masksB", eng="pool")

    S.op("dve", lambda e: e.memset(ctail[:], 0.0), writes=("ctail",))
    S.op("dve", lambda e: e.memset(stail[:], 0.0), writes=("stail",))
    S.op("dve", lambda e: e.memset(fcarry[:], 0.0), writes=("fcarry",))
    S.op("dve", lambda e: e.memset(Sst[:], 0.0), writes=("Sst",))
    S.op("dve", lambda e: e.memset(Sbf[:], 0.0), writes=("Sbf",))
    S.op("dve", lambda e: e.memset(ones4[:], 1.0), writes=("ones4",))
    S.op("act", lambda e: e.activation(out=cact[:], in_=cact[:], func=AF.Silu), reads=("cact",), writes=("cact",))
    S.op("act", lambda e: e.activation(out=Abc[:], in_=ssdp[:, :, 8:16], func=AF.Exp), reads=("ssdp",), writes=("Abc",))
    S.op("dve", lambda e: e.tensor_scalar(out=Abc[:], in0=Abc[:], scalar1=-1.0, scalar2=None, op0=ALU.mult),
         reads=("Abc",), writes=("Abc",))
    S.op("dve", lambda e: e.tensor_scalar(out=nfb[:], in0=nfb[:], scalar1=-1.0, scalar2=None, op0=ALU.mult),
         reads=("nfb",), writes=("nfb",))

    if stage in ("mix", "ln_in"):
        S.op("dve", lambda e: e.memset(mod[:], 0.0), writes=("mod",))
    with ExitStack() as ph:
        aw = [sb("aw%d" % i, [128, 8, 512], F32, ph) for i in range(2)]
        modrow = sb("modrow", [1, 9216], F32, ph)
        adab = sb("adab", [1, 9216], F32, ph)
        for l in range(DEPTH if stage not in ("mix", "ln_in") else 0):
            S.op("sp", lambda e, l=l: e.dma_start(out=adab[:], in_=adab_d[l]), writes=("adab",), dma="adab")
            for ct in range(18):
                a = aw[ct % 2]
                ra = "aw%d" % (ct % 2)
                S.op("sp", lambda e, a=a, l=l, ct=ct: e.dma_start(out=a[:], in_=adaw_d[l, ct]), writes=(ra,), dma=ra)
                for k in range(8):
                    S.op("pe", lambda e, a=a, k=k: e.matmul(PB[0][0:1, :], lhsT=cact[:, k:k + 1], rhs=a[:, k, :],
                                                            start=(k == 0), stop=(k == 7)),
                         reads=(ra, "cact"), writes=("pb0",))
                S.op("dve", lambda e, ct=ct: e.tensor_tensor(out=modrow[0:1, ct * 512:(ct + 1) * 512], in0=PB[0][0:1, :],
                                                             in1=adab[0:1, ct * 512:(ct + 1) * 512], op=ALU.add),
                     reads=("pb0", "adab"), writes=("modrow",))
            for j in range(72):
                S.op("pe", lambda e, j=j: e.matmul(PB[1][:, j:j + 1], lhsT=modrow[0:1, j * 128:(j + 1) * 128],
                                                   rhs=onesF[0:1, 0:1], start=True, stop=True),
                     reads=("modrow", "cst"), writes=("pb1",))
            S.op("dve", lambda e, l=l: e.tensor_copy(out=mod[:, l, :], in_=PB[1][:, 0:72]), reads=("pb1",), writes=("mod",))
        barrier()
    S.op("dve", lambda e: e.tensor_scalar(out=agT[:], in0=lnG[:], scalar1=float(ALPHA), scalar2=None, op0=ALU.mult),
         reads=("lnG",), writes=("agT",))
    S.op("dve", lambda e: e.tensor_scalar(out=abT[:], in0=lnB[:], scalar1=float(ALPHA), scalar2=None, op0=ALU.mult),
         reads=("lnB",), writes=("abT",))
    for n in range(6):
        l, s = n // 3, n % 3
        sh = mod[:, l, s * 24:s * 24 + 8]
        sc = mod[:, l, s * 24 + 8:s * 24 + 16]
        gt = mod[:, l, s * 24 + 16:s * 24 + 24]
        S.op("dve", lambda e, n=n, sc=sc: e.scalar_tensor_tensor(out=hsT[:, n, :], in0=sc, scalar=1.0, in1=lnG[:, n, :],
                                                                 op0=ALU.add, op1=ALU.mult),
             reads=("mod", "lnG"), writes=("hsT",))
        S.op("dve", lambda e, n=n, sc=sc: e.scalar_tensor_tensor(out=hbT[:, n, :], in0=sc, scalar=1.0, in1=lnB[:, n, :],
                                                                 op0=ALU.add, op1=ALU.mult),
             reads=("mod", "lnB"), writes=("hbT",))
        S.op("dve", lambda e, n=n, sh=sh: e.tensor_tensor(out=hbT[:, n, :], in0=hbT[:, n, :], in1=sh, op=ALU.add),
             reads=("mod", "hbT"), writes=("hbT",))
        S.op("dve", lambda e, n=n, gt=gt, s=s: e.tensor_scalar(out=gsT[:, n, :], in0=gt, scalar1=(1.0 if s == 1 else 0.5),
                                                               scalar2=None, op0=ALU.mult),
             reads=("mod",), writes=("gsT",))

    LNB = ([sb("sq%d" % i, [128, 512], F32) for i in range(2)], [sb("lt%d" % i, [128, 512], F32) for i in range(2)],
           sb("mean", [128, 512], F32), sb("var", [128, 512], F32), sb("rstd", [128, 512], F32), sb("nmr", [128, 512], F32),
           sb("xs_", [128, 512], F32), sb("qs_", [128, 512], F32))

    def ln_parts(n, final, tt):
        sq, tmp, mean, var, rstd, nmr, xs_, qs_ = LNB
        ts = slice(tt * 512, (tt + 1) * 512)

        def part_a():
            S.op("pool", lambda e, ts=ts: e.tensor_tensor(out=xs_[:], in0=xT[:, 0, ts], in1=xT[:, 1, ts], op=ALU.add),
                 reads=(("xT", 0, tt), ("xT", 1, tt)), writes=("xs_",))
            for k in range(2, 8):
                S.op("pool", lambda e, k=k, ts=ts: e.tensor_tensor(out=xs_[:], in0=xs_[:], in1=xT[:, k, ts], op=ALU.add),
                     reads=("xs_", ("xT", k, tt)), writes=("xs_",))
            S.op("act", lambda e, ts=ts: e.activation(out=qs_[:], in_=xT[:, 0, ts], func=AF.Square),
                 reads=(("xT", 0, tt),), writes=("qs_",))
            for k in range(1, 8):
                q = sq[k % 2]
                rq = "sq%d" % (k % 2)
                S.op("act", lambda e, q=q, k=k, ts=ts: e.activation(out=q[:], in_=xT[:, k, ts], func=AF.Square),
                     reads=(("xT", k, tt),), writes=(rq,))
                S.op("dve", lambda e, q=q: e.tensor_tensor(out=qs_[:], in0=qs_[:], in1=q[:], op=ALU.add),
                     reads=("qs_", rq), writes=("qs_",))

        def part_b():
            S.op("pe", lambda e: e.matmul(PB[0][:, :], lhsT=onesF, rhs=xs_[:], start=True, stop=True),
                 reads=("cst", "xs_"), writes=("pb0",))
            S.op("pe", lambda e: e.matmul(PB[1][:, :], lhsT=onesF, rhs=qs_[:], start=True, stop=True),
                 reads=("cst", "qs_"), writes=("pb1",))
            S.op("dve", lambda e: e.tensor_scalar(out=mean[:], in0=PB[0][:, :], scalar1=1.0 / D, scalar2=None, op0=ALU.mult),
                 reads=("pb0",), writes=("mean",))
            S.op("dve", lambda e: e.tensor_tensor(out=var[:], in0=mean[:], in1=mean[:], op=ALU.mult),
                 reads=("mean",), writes=("var",))
            S.op("dve", lambda e: e.scalar_tensor_tensor(out=var[:], in0=PB[1][:, :], scalar=1.0 / D, in1=var[:],
                                                         op0=ALU.mult, op1=ALU.subtract),
                 reads=("pb1", "var"), writes=("var",))
            S.op("dve", lambda e: e.tensor_scalar(out=var[:], in0=var[:], scalar1=float(LN_EPS), scalar2=None, op0=ALU.add),
                 reads=("var",), writes=("var",))
            S.op("act", lambda e: e.activation(out=rstd[:], in_=var[:], func=AF.Ln), reads=("var",), writes=("rstd",))
            S.op("act", lambda e: e.activation(out=rstd[:], in_=rstd[:], func=AF.Exp, scale=-0.5),
                 reads=("rstd",), writes=("rstd",))
            S.op("dve", lambda e: e.scalar_tensor_tensor(out=nmr[:], in0=mean[:], scalar=-1.0, in1=rstd[:],
                                                         op0=ALU.mult, op1=ALU.mult),
                 reads=("mean", "rstd"), writes=("nmr",))

        def part_c():
            for k in range(8):
                t_ = tmp[k % 2]
                rt = "lt%d" % (k % 2)
                S.op("dve", lambda e, t_=t_, k=k, ts=ts: e.tensor_tensor(out=t_[:], in0=xT[:, k, ts], in1=rstd[:], op=ALU.mult),
                     reads=(("xT", k, tt), "rstd"), writes=(rt,))
                S.op("dve", lambda e, t_=t_: e.tensor_tensor(out=t_[:], in0=t_[:], in1=nmr[:], op=ALU.add),
                     reads=(rt, "nmr"), writes=(rt,))
                if final:
                    S.op("act", lambda e, t_=t_, k=k, ts=ts: e.activation(out=xT[:, k, ts], in_=t_[:], func=AF.Identity,
                                                                          scale=lnG[:, n, k:k + 1], bias=lnB[:, n, k:k + 1]),
                         reads=(rt, "lnG", "lnB"), writes=(("xT", k, tt),))
                else:
                    S.op("act", lambda e, t_=t_, k=k, ts=ts: e.activation(out=xT[:, k, ts], in_=t_[:], func=AF.Identity,
                                                                          scale=agT[:, n, k:k + 1], bias=abT[:, n, k:k + 1]),
                         reads=(rt, "agT", "abT"), writes=(("xT", k, tt),))
                    S.op("act", lambda e, t_=t_, k=k, ts=ts: e.activation(out=hT[:, k, ts], in_=t_[:], func=AF.Identity,
                                                                          scale=hsT[:, n, k:k + 1], bias=hbT[:, n, k:k + 1]),
                         reads=(rt, "hsT", "hbT"), writes=(("hT", k, tt),))


        return part_a, part_b, part_c

    def layer_norm(n, final=False, only=None):
        for tt in range(T // 512):
            if only is not None and tt != only:
                continue
            for p_ in ln_parts(n, final, tt):
                p_()

    XT_ALL = [("xT", k, tt) for k in range(8) for tt in range(T // 512)]
    HT_ALL = [("hT", k, tt) for k in range(8) for tt in range(T // 512)]

    def ffn(l, which, n, bar, nxt):
        with ExitStack() as ph:
            actT = sb("actT", [128, NF, T], BF16, ph)
            wio = [sb("wio%d" % i, [128, 8, 256], BF16, ph) for i in range(3)]
            wo = [sb("wo%d" % i, [128, NF, 128], BF16, ph) for i in range(3)]
            sg = [sb("sg%d" % i, [128, 512], F32, ph) for i in range(2)]
            it = 0
            for f in range(NF):
                w = wio[f % 3]
                rw = "wio%d" % (f % 3)
                S.op("pool", lambda e, w=w, f=f: e.dma_start(out=w[:], in_=fwin_d[l, which, f]), writes=(rw,), dma=rw)
                for tt in range(T // 512):
                    ts = slice(tt * 512, (tt + 1) * 512)
                    bg, bu = PB[it % 2], PB[2 + it % 2]
                    rg, ru = "pb%d" % (it % 2), "pb%d" % (2 + it % 2)
                    s_ = sg[it % 2]
                    rs = "sg%d" % (it % 2)
                    it += 1
                    for k in range(8):
                        S.op("pe", lambda e, w=w, k=k, ts=ts, bg=bg: e.matmul(bg[:, :], lhsT=w[:, k, 0:128], rhs=hT[:, k, ts],
                                                                              start=(k == 0), stop=(k == 7)),
                             reads=(rw, ("hT", k, tt)), writes=(rg,))
                    for k in range(8):
                        S.op("pe", lambda e, w=w, k=k, ts=ts, bu=bu: e.matmul(bu[:, :], lhsT=w[:, k, 128:256], rhs=hT[:, k, ts],
                                                                              start=(k == 0), stop=(k == 7)),
                             reads=(rw, ("hT", k, tt)), writes=(ru,))
                    S.op("act", lambda e, s_=s_, bg=bg: e.activation(out=s_[:], in_=bg[:, :], func=AF.Silu),
                         reads=(rg,), writes=(rs,))
                    S.op("dve", lambda e, s_=s_, bu=bu, f=f, ts=ts: e.tensor_tensor(out=actT[:, f, ts], in0=s_[:], in1=bu[:, :], op=ALU.mult),
                         reads=(rs, ru), writes=(("actT", f, tt),))
            it = 0
            lnp = ln_parts(nxt[0], nxt[1], 0)
            for tt in range(T // 512):
                ts = slice(tt * 512, (tt + 1) * 512)
                for m in range(8):
                    w = wo[it % 3]
                    rw = "wo%d" % (it % 3)
                    S.op("pool", lambda e, w=w, m=m: e.dma_start(out=w[:], in_=fwout_d[l, which, m]), writes=(rw,), dma=rw)
                    by = PB[4 + it % 2]
                    ry = "pb%d" % (4 + it % 2)
                    it += 1
                    for f in range(NF):
                        S.op("pe", lambda e, w=w, f=f, ts=ts, by=by: e.matmul(by[:, :], lhsT=w[:, f, :], rhs=actT[:, f, ts],
                                                                              start=(f == 0), stop=(f == NF - 1)),
                             reads=(rw, ("actT", f, tt)), writes=(ry,))
                    S.op("dve", lambda e, m=m, ts=ts, by=by: e.scalar_tensor_tensor(out=xT[:, m, ts], in0=by[:, :], scalar=gsT[:, n, m:m + 1],
                                                                                    in1=xT[:, m, ts], op0=ALU.mult, op1=ALU.add),
                         reads=(ry, "gsT", ("xT", m, tt)), writes=(("xT", m, tt),))
                    if tt == 1 and m == 0:
                        lnp[0]()
                    if tt == 1 and m == 3:
                        lnp[1]()
                    if tt == 1 and m == 4:
                        lnp[2]()
            if bar:
                barrier()

    def mixer(l, seg, n):
        NB = T // 128
        NT = T // 512
        pc = [0]

        def proj_fm(w, rw, ncol, dst_fn, evac_eng="act"):
            for tt in range(NT):
                ts = slice(tt * 512, (tt + 1) * 512)
                bank = PB[pc[0] % 2]
                rb = "pb%d" % (pc[0] % 2)
                pc[0] += 1
                for k in range(8):
                    S.op("pe", lambda e, k=k, ts=ts, bank=bank: e.matmul(bank[0:ncol, :], lhsT=w[:, k, 0:ncol], rhs=hT[:, k, ts],
                                                                         start=(k == 0), stop=(k == 7)),
                         reads=(rw, ("hT", k, tt)), writes=(rb,))
                dst_fn(tt, ts, bank, rb)

        def wload(w, rw, src):
            S.op("pool", lambda e: e.dma_start(out=w[:], in_=src), writes=(rw,), dma=rw)

        with ExitStack() as mx:
            xbcT = sb("xbcT", [128, 8, T], BF16, mx)
            QT = sb("QT", [128, 4, T], BF16, mx)
            KT = sb("KT", [128, 4, T], BF16, mx)
            Vo = sb("Vo", [128, NB, 4, 64], BF16, mx)
            yT = sb("yT", [128, 8, T], BF16, mx)
            with ExitStack() as ph:
                stg = [sb("stg%d" % i, [128, 3 + T], F32, ph) for i in range(2)]
                acc = [sb("acc%d" % i, [128, T], F32, ph) for i in range(2)]
                wx = [sb("wx%d" % i, [128, 8, 128], BF16, ph) for i in range(3)]
                wq = [sb("wq%d" % i, [128, 8, 64], BF16, ph) for i in range(3)]
                wfs = sb("wfs", [128, 8, 4], BF16, ph)
                wvs = sb("wvs", [128, 8, 256], BF16, ph)
                fT = sb("fT", [4, T], F32, ph)
                cum = sb("cum", [4, T], F32, ph)
                crow = sb("crow", [4, T], BF16, ph)
                scB = sb("scB", [128, T], F32, ph)
                scC = sb("scC", [128, T], F32, ph)
                scU = sb("scU", [128, 2 + T], F32, ph)
                scA = sb("scA", [128, T], F32, ph)
                for i in range(8):
                    w, rw = wx[i % 3], "wx%d" % (i % 3)
                    wload(w, rw, wxbc_d[l, i])
                    st, rst = stg[i % 2], "stg%d" % (i % 2)
                    a, ra = acc[i % 2], "acc%d" % (i % 2)
                    S.op("dve", lambda e, st=st, i=i: e.tensor_copy(out=st[:, 0:3], in_=ctail[:, l, i, :]),
                         reads=("ctail",), writes=(rst,))

                    def ev(tt, ts, bank, rb, st=st, rst=rst):
                        S.op("act", lambda e: e.activation(out=st[:, 3 + tt * 512:3 + (tt + 1) * 512], in_=bank[:, :], func=AF.Identity),
                             reads=(rb,), writes=(rst,))
                    proj_fm(w, rw, 128, ev)
                    S.op("dve", lambda e, st=st, a=a, i=i: e.tensor_scalar(out=a[:], in0=st[:, 0:T], scalar1=cw[:, l, i, 0:1], scalar2=None, op0=ALU.mult),
                         reads=(rst, "cw"), writes=(ra,))
                    for j in range(1, 4):
                        S.op("dve", lambda e, st=st, a=a, i=i, j=j: e.scalar_tensor_tensor(out=a[:], in0=st[:, j:j + T], scalar=cw[:, l, i, j:j + 1],
                                                                                           in1=a[:], op0=ALU.mult, op1=ALU.add),
                             reads=(rst, "cw", ra), writes=(ra,))
                    S.op("dve", lambda e, st=st, i=i: e.tensor_copy(out=ctail[:, l, i, :], in_=st[:, T:T + 3]),
                         reads=(rst,), writes=("ctail",))
                    S.op("act", lambda e, a=a, i=i: e.activation(out=xbcT[:, i, :], in_=a[:], func=AF.Silu, bias=cbias[:, l, i:i + 1]),
                         reads=(ra, "cbias"), writes=(("xbcT", i),))
                for j in range(8):
                    w, rw = wq[j % 3], "wq%d" % (j % 3)
                    wload(w, rw, wqk_d[l, j])
                    dstT, rd = (QT, "QT") if j < 4 else (KT, "KT")

                    def ev(tt, ts, bank, rb, dstT=dstT, rd=rd, j=j):
                        S.op("act", lambda e: e.activation(out=dstT[0:64, j % 4, ts], in_=bank[0:64, :], func=AF.Identity),
                             reads=(rb,), writes=((rd, j % 4),))
                    proj_fm(w, rw, 64, ev)
                S.op("dve", lambda e: e.memset(KT[64:65, :, :], 1.0), writes=tuple(("KT", h) for h in range(4)))
                wload(wfs, "wfs", wf_d[l])

                def ev(tt, ts, bank, rb):
                    S.op("act", lambda e: e.activation(out=fT[:, ts], in_=bank[0:4, :], func=AF.Exp, scale=-1.0, bias=nfb[:, l, :]),
                         reads=(rb, "nfb"), writes=("fT",))
                proj_fm(wfs, "wfs", 4, ev)
                S.op("act", lambda e: e.activation(out=fT[:], in_=fT[:], func=AF.Ln, bias=1.0), reads=("fT",), writes=("fT",))
                S.op("dve", lambda e: e.tensor_tensor_scan(out=cum[:], data0=ones4[:], data1=fT[:], initial=fcarry[:, l, :],
                                                           op0=ALU.mult, op1=ALU.add),
                     reads=("fT", "ones4", "fcarry"), writes=("cum",))
                S.op("dve", lambda e: e.tensor_copy(out=fcarry[:, l, :], in_=cum[:, T - 1:T]), reads=("cum",), writes=("fcarry",))
                S.op("act", lambda e: e.activation(out=crow[:], in_=cum[:], func=AF.Identity, scale=-8.0), reads=("cum",), writes=("crow",))
                for h in range(4):
                    S.op("sp", lambda e, h=h: e.dma_start(out=QT[64:65, h, :], in_=crow[h:h + 1, :]),
                         reads=("crow",), writes=(("QT", h),), dma="crow%d" % h)
                for blk in range(NB):
                    S.op("pe", lambda e, blk=blk: e.transpose(out=PB[5][:, blk * 4:(blk + 1) * 4], in_=cum[0:4, blk * 128:(blk + 1) * 128],
                                                              identity=identF[0:4, 0:4]),
                         reads=("cum", "cst"), writes=("pb5",))
                S.op("dve", lambda e: e.tensor_copy(out=kbias[:, l, seg * NB:(seg + 1) * NB, :],
                                                    in_=PB[5][:, 0:4 * NB].rearrange("p (a b) -> p a b", b=4)),
                     reads=("pb5",), writes=("kbias",))
                wload(wvs, "wvs", wv_d[l])
                for blk in range(NB):
                    bank, rb = PB[2 + blk % 2], "pb%d" % (2 + blk % 2)
                    tsl = slice(blk * 128, (blk + 1) * 128)
                    for k in range(8):
                        S.op("pe", lambda e, k=k, tsl=tsl, bank=bank: e.matmul(bank[:, 0:256], lhsT=hT[:, k, tsl], rhs=wvs[:, k, :],
                                                                               start=(k == 0), stop=(k == 7)),
                             reads=("wvs", ("hT", k, blk // 4)), writes=(rb,))
                    S.op("act", lambda e, blk=blk, bank=bank: e.activation(out=Vo[:, blk, :, :].rearrange("p a b -> p (a b)"), in_=bank[:, 0:256],
                                                                           func=AF.Identity),
                         reads=(rb,), writes=("Vo",))
                for i in range(2):
                    for which, dst, rd in ((0, scB, "scB"), (1, scC, "scC")):
                        w, rw = wx[pc[0] % 3], "wx%d" % (pc[0] % 3)
                        wload(w, rw, wsc_d[l, which * 2 + i])

                        def ev(tt, ts, bank, rb, dst=dst, rd=rd):
                            S.op("act", lambda e: e.activation(out=dst[:, ts], in_=bank[:, :], func=AF.Identity),
                                 reads=(rb,), writes=(rd,))
                        proj_fm(w, rw, 128, ev)
                    S.op("dve", lambda e, i=i: e.tensor_copy(out=scU[:, 0:2], in_=stail[:, l, i, :]), reads=("stail",), writes=("scU",))
                    w, rw = wx[pc[0] % 3], "wx%d" % (pc[0] % 3)
                    wload(w, rw, wsc_d[l, 4 + i])

                    def ev(tt, ts, bank, rb):
                        S.op("dve", lambda e: e.tensor_tensor(out=scU[:, 2 + tt * 512:2 + (tt + 1) * 512], in0=bank[:, :], in1=scC[:, ts], op=ALU.mult),
                             reads=(rb, "scC"), writes=("scU",))
                    proj_fm(w, rw, 128, ev)
                    S.op("dve", lambda e, i=i: e.tensor_scalar(out=scA[:], in0=scU[:, 0:T], scalar1=scw[:, l, i, 0:1], scalar2=None, op0=ALU.mult),
                         reads=("scU", "scw"), writes=("scA",))
                    for j in range(1, 3):
                        S.op("dve", lambda e, i=i, j=j: e.scalar_tensor_tensor(out=scA[:], in0=scU[:, j:j + T], scalar=scw[:, l, i, j:j + 1],
                                                                               in1=scA[:], op0=ALU.mult, op1=ALU.add),
                             reads=("scU", "scw", "scA"), writes=("scA",))
                    S.op("dve", lambda e, i=i: e.tensor_copy(out=stail[:, l, i, :], in_=scU[:, T:T + 2]), reads=("scU",), writes=("stail",))
                    S.op("dve", lambda e, i=i: e.tensor_tensor(out=yT[:, 6 + i, :], in0=scA[:], in1=scB[:], op=ALU.mult),
                         reads=("scA", "scB"), writes=(("yT", 6 + i),))
                for h in range(4):
                    S.op("sp", lambda e, h=h: e.dma_start(out=kh_d[l, seg, h], in_=KT[0:65, h, :]),
                         reads=(("KT", h),), writes=(("kh", l, seg, h),), dma="hwk%d" % h)
                    S.op("sp", lambda e, h=h: e.dma_start(out=vh_d[l, seg, h], in_=Vo[:, :, h, :]),
                         reads=("Vo",), writes=(("vh", l, seg, h),), dma="hwv%d" % h)
                barrier()
            with ExitStack() as ph:
                Kb = [sb("Kb%d" % i, [128, T], BF16, ph) for i in range(2)]
                Vb = [sb("Vb%d" % i, [128, NB, 2, 64], BF16, ph) for i in range(2)]
                Pb = [sb("Pb%d" % i, [128, 512], BF16, ph) for i in range(3)]
                rec = sb("rec", [128, 512], F32, ph)
                for i in range(2):
                    S.op("dve", lambda e, i=i: e.memset(Vb[i][:, :, 1, :], 1.0), writes=("Vb%d" % i,))
                li = 0
                items = []
                for h in range(4):
                    ob = (0, 1) if h % 2 == 0 else (5, 6)
                    for ks in range(seg + 1):
                        slot = li % 2
                        li += 1
                        newload = True
                        for blk in range(NB):
                            for q in range(NT):
                                if ks == seg and blk >= 4 * (q + 1):
                                    continue
                                items.append(dict(h=h, ks=ks, blk=blk, q=q, slot=slot, load=newload, ob=ob[q],
                                                  diag=(ks == seg and blk >= 4 * q), first=(ks == 0 and blk == 0),
                                                  last=(ks == seg and blk == 4 * q + 3)))
                                newload = False

                def emit_s(i, it):
                    h, ks, blk, q, slot = it["h"], it["ks"], it["blk"], it["q"], it["slot"]
                    kb_, vb_ = Kb[slot], Vb[slot]
                    rk, rv = "Kb%d" % slot, "Vb%d" % slot
                    if it["load"]:
                        S.op("sp", lambda e: e.dma_start(out=kb_[0:65, :], in_=kh_d[l, ks, h]),
                             reads=(("kh", l, ks, h),), writes=(rk,), dma=rk)
                        S.op("sp", lambda e: e.dma_start(out=vb_[:, :, 0, :], in_=vh_d[l, ks, h]),
                             reads=(("vh", l, ks, h),), writes=(rv,), dma=rv)
                    Sb, rs = PB[2 + i % 3], "pb%d" % (2 + i % 3)
                    p_, rp = Pb[i % 3], "Pb%d" % (i % 3)
                    diag = it["diag"]
                    S.op("pe", lambda e: e.matmul(Sb[:, :], lhsT=kb_[0:65, blk * 128:(blk + 1) * 128], rhs=QT[0:65, h, q * 512:(q + 1) * 512],
                                                  start=True, stop=(not diag)), reads=(rk, ("QT", h)), writes=(rs,))
                    if diag:
                        S.op("pe", lambda e: e.matmul(Sb[:, :], lhsT=identB[:], rhs=masksB[:, blk - 4 * q, :], start=False, stop=True),
                             reads=("identB", "masksB"), writes=(rs,))
                    S.op("act", lambda e: e.activation(out=p_[:], in_=Sb[:, :], func=AF.Exp, scale=0.125, bias=kbias[:, l, ks * NB + blk, h:h + 1]),
                         reads=(rs, "kbias"), writes=(rp,))

                def emit_pv(i, it):
                    h, blk, q, slot = it["h"], it["blk"], it["q"], it["slot"]
                    vb_, rv = Vb[slot], "Vb%d" % slot
                    p_, rp = Pb[i % 3], "Pb%d" % (i % 3)
                    O_ = PB[it["ob"]]
                    ro = "pb%d" % it["ob"]
                    S.op("pe", lambda e: e.matmul(O_[:, :], lhsT=vb_[:, blk, :, :].rearrange("p a b -> p (a b)"), rhs=p_[:],
                                                  start=it["first"], stop=it["last"]), reads=(rv, rp), writes=(ro,))
                    if it["last"]:
                        S.op("dve", lambda e: e.reciprocal(out=rec[64:128, :], in_=O_[64:128, :]), reads=(ro,), writes=("rec",))
                        S.op("dve", lambda e: e.tensor_tensor(out=yT[(h % 2) * 64:(h % 2) * 64 + 64, 4 + h // 2, q * 512:(q + 1) * 512],
                                                              in0=O_[0:64, :], in1=rec[64:128, :], op=ALU.mult),
                             reads=(ro, "rec"), writes=(("yT", 4 + h // 2),))

                LOOK = 2
                for i in range(len(items) + LOOK):
                    if i < len(items):
                        emit_s(i, items[i])
                    if i >= LOOK:
                        emit_pv(i - LOOK, items[i - LOOK])
                barrier()
            with ExitStack() as ph:
                wzs = sb("wzs", [128, 8, 512], BF16, ph)
                wdts = sb("wdts", [128, 8, 8], BF16, ph)
                wload(wzs, "wzs", wz_d[l])
                wload(wdts, "wdts", wdt_d[l])
                dtr = sb("dtr", [128, 8], F32, ph)
                dt_ = sb("dt_", [128, 8], F32, ph)
                a_ = sb("a_", [128, 8], F32, ph)
                acs = sb("acs", [128, 8], F32, ph)
                ea = sb("ea", [128, 8], F32, ph)
                etot = sb("etot", [128, 8], F32, ph)
                dend = sb("dend", [128, 8], F32, ph)
                aU = sb("aU", [128, 8, 128], F32, ph)
                LT = [sb("LT%d" % g, [128, 4, 128], F32, ph) for g in range(2)]
                MT = [sb("MT%d" % g, [128, 4, 128], BF16, ph) for g in range(2)]
                xsB = sb("xsB", [128, 768], BF16, ph)
                cbm = sb("cbm", [128, 2, 128], F32, ph)
                xdt = sb("xdt", [128, 8, 64], BF16, ph)
                xdd = sb("xdd", [128, 8, 64], BF16, ph)
                y1 = sb("y1", [128, 8, 64], F32, ph)
                y2 = sb("y2", [128, 8, 64], F32, ph)
                sz = sb("sz", [128, 512], F32, ph)
                yz = sb("yz", [128, 512], F32, ph)
                junk = sb("junk", [128, 512], F32, ph)
                ss = sb("ss", [128, 2], F32, ph)
                rt = sb("rt", [128, 2], F32, ph)
                yn = sb("yn", [128, 512], BF16, ph)
                Sv = Sst[:, l, :].rearrange("p (h d) -> p h d", d=64)
                ea2 = [ea, sb("ea_b", [128, 8], F32, ph)]
                etot2 = [etot, sb("etot_b", [128, 8], F32, ph)]
                MT2 = [MT, [sb("MTb%d" % g, [128, 4, 128], BF16, ph) for g in range(2)]]
                xsB2 = [xsB, sb("xsB_b", [128, 768], BF16, ph)]
                xdt2 = [xdt, sb("xdt_b", [128, 8, 64], BF16, ph)]
                xdd2 = [xdd, sb("xdd_b", [128, 8, 64], BF16, ph)]

                def bc(ap2, n_):
                    return ap2.unsqueeze(2).to_broadcast([128, ap2.shape[1], n_])

                def prep(c):
                    pb = c % 2
                    sfx = "_%d" % pb
                    ea_, etot_, MT_, xsB_, xdt_, xdd_ = ea2[pb], etot2[pb], MT2[pb], xsB2[pb], xdt2[pb], xdd2[pb]
                    tsl = slice(c * 128, (c + 1) * 128)
                    tt = c // 4
                    for k in range(8):
                        S.op("pe", lambda e, k=k: e.matmul(PB[0][:, 0:8], lhsT=hT[:, k, tsl], rhs=wdts[:, k, :], start=(k == 0), stop=(k == 7)),
                             reads=("wdts", ("hT", k, tt)), writes=("pb0",))
                    yield
                    S.op("dve", lambda e: e.tensor_tensor(out=dtr[:], in0=PB[0][:, 0:8], in1=ssdp[:, l, 0:8], op=ALU.add),
                         reads=("pb0", "ssdp"), writes=("dtr",))
                    yield
                    S.op("act", lambda e: e.activation(out=dtr[:], in_=dtr[:], func=AF.Exp), reads=("dtr",), writes=("dtr",))
                    S.op("act", lambda e: e.activation(out=dt_[:], in_=dtr[:], func=AF.Ln, bias=1.0), reads=("dtr",), writes=("dt_",))
                    yield
                    S.op("dve", lambda e: e.tensor_tensor(out=a_[:], in0=dt_[:], in1=Abc[:, l, :], op=ALU.mult), reads=("dt_", "Abc"), writes=("a_",))
                    yield
                    S.op("pe", lambda e: e.matmul(PB[0][:, 8:16], lhsT=Umat, rhs=a_[:], start=True, stop=True), reads=("cst", "a_"), writes=("pb0",))
                    S.op("pe", lambda e: e.matmul(PB[0][:, 16:24], lhsT=onesF, rhs=a_[:], start=True, stop=True), reads=("cst", "a_"), writes=("pb0",))
                    yield
                    S.op("act", lambda e: e.activation(out=ea_[:], in_=PB[0][:, 8:16], func=AF.Exp), reads=("pb0",), writes=("ea" + sfx,))
                    S.op("act", lambda e: e.activation(out=etot_[:], in_=PB[0][:, 16:24], func=AF.Exp), reads=("pb0",), writes=("etot" + sfx,))
                    S.op("dve", lambda e: e.tensor_copy(out=acs[:], in_=PB[0][:, 8:16]), reads=("pb0",), writes=("acs",))
                    yield
                    S.op("dve", lambda e: e.tensor_tensor(out=dend[:], in0=PB[0][:, 16:24], in1=acs[:], op=ALU.subtract),
                         reads=("pb0", "acs"), writes=("dend",))
                    yield
                    S.op("act", lambda e: e.activation(out=dend[:], in_=dend[:], func=AF.Exp), reads=("dend",), writes=("dend",))
                    S.op("dve", lambda e: e.tensor_tensor(out=aU[:], in0=Umat.unsqueeze(1).to_broadcast([128, 8, 128]), in1=bc(a_[:], 128), op=ALU.mult),
                         reads=("cst", "a_"), writes=("aU",))
                    yield
                    for i in range(6):
                        S.op("pe", lambda e, i=i: e.transpose(out=PT[:, i * 128:(i + 1) * 128], in_=xbcT[:, i, tsl], identity=identB[:]),
                             reads=(("xbcT", i), "identB"), writes=("pt",))
                    yield
                    S.op("dve", lambda e: e.tensor_copy(out=xsB_[:], in_=PT[:, 0:768]), reads=("pt",), writes=("xsB" + sfx,))
                    for g in range(2):
                        S.op("pe", lambda e, g=g: e.matmul(PB[0][:, 128 + g * 128:128 + (g + 1) * 128], lhsT=xbcT[:, 4 + g, tsl], rhs=xbcT[:, 6 + g, tsl],
                                                           start=True, stop=True),
                             reads=(("xbcT", 4 + g), ("xbcT", 6 + g)), writes=("pb0e",))
                    yield
                    S.op("dve", lambda e: e.tensor_tensor(out=cbm[:], in0=PB[0][:, 128:384].rearrange("p (a b) -> p a b", b=128),
                                                          in1=maskT.unsqueeze(1).to_broadcast([128, 2, 128]), op=ALU.mult),
                         reads=("pb0e", "cst"), writes=("cbm",))
                    yield
                    for g in range(2):
                        for e4 in range(4):
                            S.op("pe", lambda e, g=g, e4=e4: e.matmul(PB[1 + g][:, e4 * 128:(e4 + 1) * 128], lhsT=Msm, rhs=aU[:, 4 * g + e4, :],
                                                                       start=True, stop=True),
                                 reads=("cst", "aU"), writes=("pb%d" % (1 + g),))
                        yield
                        S.op("act", lambda e, g=g: e.activation(out=LT[g][:].rearrange("p a b -> p (a b)"), in_=PB[1 + g][:, :], func=AF.Exp),
                             reads=("pb%d" % (1 + g),), writes=("LT%d" % g,))
                        yield
                    xs3 = xsB_[:, 0:512].rearrange("p (h d) -> p h d", d=64)
                    S.op("dve", lambda e: e.tensor_tensor(out=xdt_[:], in0=xs3, in1=bc(dt_[:], 64), op=ALU.mult),
                         reads=("xsB" + sfx, "dt_"), writes=("xdt" + sfx,))
                    yield
                    S.op("dve", lambda e: e.tensor_tensor(out=xdd_[:], in0=xdt_[:], in1=bc(dend[:], 64), op=ALU.mult),
                         reads=("xdt" + sfx, "dend"), writes=("xdd" + sfx,))
                    yield
                    for g in range(2):
                        S.op("dve", lambda e, g=g: e.tensor_tensor(out=MT_[g][:], in0=LT[g][:], in1=cbm[:, g:g + 1, :].to_broadcast([128, 4, 128]), op=ALU.mult),
                             reads=("LT%d" % g, "cbm"), writes=("MT%d%s" % (g, sfx),))
                        yield

                def post(c):
                    pb = c % 2
                    sfx = "_%d" % pb
                    ea_, etot_, MT_, xsB_, xdt_, xdd_ = ea2[pb], etot2[pb], MT2[pb], xsB2[pb], xdt2[pb], xdd2[pb]
                    tsl = slice(c * 128, (c + 1) * 128)
                    tt = c // 4
                    xs3 = xsB_[:, 0:512].rearrange("p (h d) -> p h d", d=64)
                    for h in range(8):
                        S.op("pe", lambda e, h=h: e.matmul(PB[4][:, h * 64:(h + 1) * 64], lhsT=MT_[h // 4][:, h % 4, :], rhs=xdt_[:, h, :],
                                                           start=True, stop=True),
                             reads=("MT%d%s" % (h // 4, sfx), "xdt" + sfx), writes=("pb4",))
                    for g in range(2):
                        S.op("pe", lambda e, g=g: e.matmul(PB[5][:, g * 256:(g + 1) * 256], lhsT=xbcT[:, 6 + g, tsl],
                                                           rhs=Sbf[:, l, g * 256:(g + 1) * 256], start=True, stop=True),
                             reads=(("xbcT", 6 + g), "Sbf"), writes=("pb5",))
                    for g in range(2):
                        S.op("pe", lambda e, g=g: e.matmul(PB[6][:, g * 256:(g + 1) * 256], lhsT=xsB_[:, 512 + g * 128:512 + (g + 1) * 128],
                                                           rhs=xdd_[:, 4 * g:4 * g + 4, :].rearrange("p a b -> p (a b)"), start=True, stop=True),
                             reads=("xsB" + sfx, "xdd" + sfx), writes=("pb6",))
                    yield
                    S.op("dve", lambda e: e.tensor_tensor(out=y1[:], in0=PB[5][:, :].rearrange("p (h d) -> p h d", d=64), in1=bc(ea_[:], 64), op=ALU.mult),
                         reads=("pb5", "ea" + sfx), writes=("y1",))
                    yield
                    S.op("dve", lambda e: e.tensor_tensor(out=Sv, in0=Sv, in1=bc(etot_[:], 64), op=ALU.mult), reads=("Sst", "etot" + sfx), writes=("Sst",))
                    yield
                    S.op("dve", lambda e: e.tensor_tensor(out=Sst[:, l, :], in0=Sst[:, l, :], in1=PB[6][:, :], op=ALU.add),
                         reads=("Sst", "pb6"), writes=("Sst",))
                    yield
                    S.op("act", lambda e: e.activation(out=Sbf[:, l, :], in_=Sst[:, l, :], func=AF.Identity), reads=("Sst",), writes=("Sbf",))
                    S.op("dve", lambda e: e.tensor_tensor(out=y1[:], in0=y1[:], in1=PB[4][:, :].rearrange("p (h d) -> p h d", d=64), op=ALU.add),
                         reads=("pb4", "y1"), writes=("y1",))
                    yield
                    S.op("dve", lambda e: e.tensor_tensor(out=y2[:], in0=xs3, in1=bc(ssdp[:, l, 16:24], 64), op=ALU.mult),
                         reads=("xsB" + sfx, "ssdp"), writes=("y2",))
                    for k in range(8):
                        S.op("pe", lambda e, k=k: e.matmul(PB[3][:, :], lhsT=hT[:, k, tsl], rhs=wzs[:, k, :], start=(k == 0), stop=(k == 7)),
                             reads=("wzs", ("hT", k, tt)), writes=("pb3",))
                    yield
                    S.op("dve", lambda e: e.tensor_tensor(out=y1[:], in0=y1[:], in1=y2[:], op=ALU.add), reads=("y1", "y2"), writes=("y1",))
                    S.op("act", lambda e: e.activation(out=sz[:], in_=PB[3][:, :], func=AF.Silu), reads=("pb3",), writes=("sz",))
                    yield
                    S.op("dve", lambda e: e.tensor_tensor(out=yz[:], in0=y1[:].rearrange("p a b -> p (a b)"), in1=sz[:], op=ALU.mult),
                         reads=("y1", "sz"), writes=("yz",))
                    yield
                    for g in range(2):
                        S.op("act", lambda e, g=g: e.activation(out=junk[:, g * 256:(g + 1) * 256], in_=yz[:, g * 256:(g + 1) * 256], func=AF.Square,
                                                                accum_out=ss[:, g:g + 1]),
                             reads=("yz",), writes=("junk", "ss"))
                    yield
                    S.op("dve", lambda e: e.tensor_scalar(out=rt[:], in0=ss[:], scalar1=1.0 / 256, scalar2=float(RMS_EPS), op0=ALU.mult, op1=ALU.add),
                         reads=("ss",), writes=("rt",))
                    yield
                    S.op("act", lambda e: e.activation(out=rt[:], in_=rt[:], func=AF.Ln), reads=("rt",), writes=("rt",))
                    S.op("act", lambda e: e.activation(out=rt[:], in_=rt[:], func=AF.Exp, scale=-0.5), reads=("rt",), writes=("rt",))
                    yield
                    for g in range(2):
                        S.op("dve", lambda e, g=g: e.scalar_tensor_tensor(out=yn[:, g * 256:(g + 1) * 256], in0=yz[:, g * 256:(g + 1) * 256],
                                                                          scalar=rt[:, g:g + 1], in1=normg[:, l, g * 256:(g + 1) * 256],
                                                                          op0=ALU.mult, op1=ALU.mult),
                             reads=("yz", "rt", "normg"), writes=("yn",))
                        yield
                    for half in range(2):
                        for i in range(2):
                            S.op("pe", lambda e, i=i, half=half: e.transpose(out=PT[:, 768 + i * 128:768 + (i + 1) * 128],
                                                                            in_=yn[:, (2 * half + i) * 128:(2 * half + i + 1) * 128], identity=identB[:]),
                                 reads=("yn", "identB"), writes=("pt2",))
                        yield
                        S.op("dve", lambda e, half=half: e.tensor_copy(out=yT[:, 2 * half:2 * half + 2, tsl],
                                                                      in_=PT[:, 768:1024].rearrange("p (a b) -> p a b", b=128)),
                             reads=("pt2",), writes=(("yT", 2 * half), ("yT", 2 * half + 1)))
                        yield

                def run_il(gens):
                    gens = [g for g in gens if g is not None]
                    while gens:
                        for g in list(gens):
                            try:
                                next(g)
                            except StopIteration:
                                gens.remove(g)

                run_il([prep(0)])
                for c in range(NB):
                    run_il([post(c), prep(c + 1) if c + 1 < NB else None])
                barrier()
            if stage == "mix":
                dump("yT%d" % seg, yT[:], [128, 8, T], BF16, res=tuple(("yT", i) for i in range(8)))
            with ExitStack() as ph:
                wm = [sb("wm%d" % i, [128, 8, 128], BF16, ph) for i in range(3)]
                it = 0
                for m in range(8):
                    w, rw = wm[m % 3], "wm%d" % (m % 3)
                    wload(w, rw, wmo_d[l, m])
                    for tt in range(NT):
                        ts = slice(tt * 512, (tt + 1) * 512)
                        bank, rb = PB[it % 2], "pb%d" % (it % 2)
                        it += 1
                        for k in range(8):
                            S.op("pe", lambda e, w=w, k=k, ts=ts, bank=bank: e.matmul(bank[:, :], lhsT=w[:, k, :], rhs=yT[:, k, ts],
                                                                                      start=(k == 0), stop=(k == 7)),
                                 reads=(rw, ("yT", k)), writes=(rb,))
                        S.op("dve", lambda e, m=m, ts=ts, bank=bank: e.scalar_tensor_tensor(out=xT[:, m, ts], in0=bank[:, :], scalar=gsT[:, n, m:m + 1],
                                                                                            in1=xT[:, m, ts], op0=ALU.mult, op1=ALU.add),
                             reads=(rb, "gsT", ("xT", m, tt)), writes=(("xT", m, tt),))
                barrier()

    for seg in range(nseg):
        for k in range(8):
            S.op("sp", lambda e, k=k, seg=seg: e.dma_start(out=xT[:, k, :], in_=xT_d[seg, :, k, :]),
                 writes=(("xT", k, 0), ("xT", k, 1)), dma="xin%d" % k)
        layer_norm(0)
        if stage == "ln_in":
            dump("xT", xT[:], [128, 8, T], res=XT_ALL)
            dump("hT", hT[:], [128, 8, T], BF16, res=HT_ALL)
            break
        if stage == "mix":
            mixer(0, seg, 1)
            continue
        for l in range(depth):
            ffn(l, 0, 3 * l, True, (1 + 3 * l, False))
            layer_norm(1 + 3 * l, only=1)
            if stage == "ffn1":
                dump("xT", xT[:], [128, 8, T], res=XT_ALL)
                break
            mixer(l, seg, 3 * l + 1)
            layer_norm(2 + 3 * l)
            ffn(l, 1, 3 * l + 2, False, (3 + 3 * l, l == depth - 1))
            layer_norm(3 + 3 * l, final=(l == depth - 1), only=1)
        if stage != "full":
            break
        for k in range(8):
            S.op("sp", lambda e, k=k, seg=seg: e.dma_start(out=out_d[seg, :, k, :], in_=xT[:, k, :]),
                 reads=(("xT", k, 0), ("xT", k, 1)), writes=("outd",), dma="xout%d" % k)

    info = S.emit()
    return nc, es, info, dump_specs


def pack_inputs(inp, b):
    f32 = np.float32
    m = {}
    x = np.asarray(inp["x"][b], f32)
    m["xT"] = np.ascontiguousarray(x.reshape(NSEG, T, 8, 128).transpose(0, 3, 2, 1))
    m["c_r"] = _fm(inp["c"][b])
    m["consts"] = _consts()
    lnG = [inp["ln_in_g"]] + [inp["ln_g"][l, s] for l in range(DEPTH) for s in range(3)]
    lnB = [inp["ln_in_b"]] + [inp["ln_b"][l, s] for l in range(DEPTH) for s in range(3)]
    m["lnG"] = np.ascontiguousarray(np.concatenate([_fm(v) for v in lnG], axis=1))
    m["lnB"] = np.ascontiguousarray(np.concatenate([_fm(v) for v in lnB], axis=1))
    aw = np.asarray(inp["ada_w"], f32)
    m["adaw"] = np.ascontiguousarray(aw.reshape(DEPTH, 8, 128, 18, 512).transpose(0, 3, 2, 1, 4))
    m["adab"] = np.ascontiguousarray(np.asarray(inp["ada_b"], f32).reshape(DEPTH, 1, 9216))
    fwin = np.zeros((DEPTH, 2, NF, 128, 8, 256), f32)
    fwout = np.zeros((DEPTH, 2, 8, 128, NF, 128), f32)
    for wi, (kin, kout) in enumerate((("ffn1_w_in", "ffn1_w_out"), ("ffn2_w_in", "ffn2_w_out"))):
        for l in range(DEPTH):
            w = np.asarray(inp[kin][l], f32).reshape(8, 128, 2, NF, 128)
            fwin[l, wi] = w.transpose(3, 1, 0, 2, 4).reshape(NF, 128, 8, 256)
            wo = np.asarray(inp[kout][l], f32).reshape(NF, 128, 8, 128)
            fwout[l, wi] = wo.transpose(2, 1, 0, 3)
    m["fwin"], m["fwout"] = fwin, fwout
    mw = np.asarray(inp["mix_w_in"], f32)

    def cols(l, lo, hi):
        return _kmaj(mw[l][:, lo:hi])
    m["wxbc"] = np.stack([np.stack([cols(l, 512 + 128 * i, 512 + 128 * (i + 1)) for i in range(8)]) for l in range(DEPTH)])
    m["wqk"] = np.stack([np.stack([cols(l, 1544 + 64 * i, 1544 + 64 * (i + 1)) for i in range(8)]) for l in range(DEPTH)])
    m["wsc"] = np.stack([np.stack([cols(l, 2316 + 128 * i, 2316 + 128 * (i + 1)) for i in range(6)]) for l in range(DEPTH)])
    m["wf"] = np.stack([cols(l, 2312, 2316) for l in range(DEPTH)])
    m["wz"] = np.stack([cols(l, 0, 512) for l in range(DEPTH)])
    m["wdt"] = np.stack([cols(l, 1536, 1544) for l in range(DEPTH)])
    m["wv"] = np.stack([cols(l, 2056, 2312) for l in range(DEPTH)])
    wmo = np.asarray(inp["mix_w_out"], f32)
    m["wmo"] = np.ascontiguousarray(wmo.reshape(DEPTH, 8, 128, 8, 128).transpose(0, 3, 2, 1, 4))
    cwv = np.asarray(inp["ssd_conv_w"], f32)
    m["cw"] = np.ascontiguousarray(cwv.reshape(DEPTH, 4, 8, 128).transpose(0, 3, 2, 1).reshape(DEPTH, 128, 32))
    m["cb"] = np.stack([_fm(inp["ssd_conv_b"][l]) for l in range(DEPTH)])
    sp = np.concatenate([np.asarray(inp["ssd_dt_bias"], f32), np.asarray(inp["ssd_a_log"], f32),
                         np.asarray(inp["ssd_d"], f32)], axis=1)
    m["ssdp"] = np.ascontiguousarray(np.broadcast_to(sp[:, None, :], (DEPTH, 128, 24)))
    m["normg"] = np.ascontiguousarray(np.broadcast_to(np.asarray(inp["ssd_norm_g"], f32)[:, None, :], (DEPTH, 128, 512)))
    m["fb"] = np.ascontiguousarray(np.asarray(inp["fox_f_bias"], f32).reshape(DEPTH, 4, 1))
    scwv = np.asarray(inp["sconv_w"], f32)
    m["scw"] = np.ascontiguousarray(scwv.reshape(DEPTH, 3, 2, 128).transpose(0, 3, 2, 1).reshape(DEPTH, 128, 6))
    return {k: np.ascontiguousarray(v, dtype=f32) for k, v in m.items()}


_CACHE = {}


def kernel(**inputs):
    if "nc" not in _CACHE:
        _CACHE["nc"] = build("full")
    nc, es, info, _ = _CACHE["nc"]
    in_maps = [pack_inputs(inputs, b) for b in range(NCORES)]
    res = run_bass_kernel_spmd(nc, in_maps, core_ids=list(range(NCORES)))
    out = np.zeros((NCORES, SEQ, D), np.float32)
    for b in range(NCORES):
        o = res.results[b]["outT"]
        out[b] = o.transpose(0, 3, 2, 1).reshape(SEQ, D)
    return out
```

```python
import numpy as np
from contextlib import ExitStack
import concourse.bass as bass
import concourse.mybir as mybir
from concourse.bass_utils import run_bass_kernel_spmd

F32 = mybir.dt.float32
F32R = mybir.dt.float32r
BF16 = mybir.dt.bfloat16
AF = mybir.ActivationFunctionType
ALU = mybir.AluOpType

D = 1024
SEQ = 4096
T = 1024
NSEG = SEQ // T
DFF = 2816
NF = DFF // 128
DEPTH = 2
ALPHA = (2 * DEPTH) ** 0.25
LN_EPS = 1e-5
RMS_EPS = 1e-5
NCORES = 4
NEG = -1.0e5


class Op:
    __slots__ = ("eng", "fn", "deps", "dma", "sig", "val", "idx", "force")


class Sched:
    def __init__(self, nc, es):
        self.nc = nc
        self.es = es
        self.h = {"pe": nc.tensor, "act": nc.scalar, "dve": nc.vector, "pool": nc.gpsimd, "sp": nc.sync}
        self.ops = []
        self.last_w = {}
        self.readers = {}
        self.streams = {}

    def op(self, eng, fn, reads=(), writes=(), dma=None):
        o = Op()
        o.eng, o.fn, o.dma, o.sig, o.val = eng, fn, dma, False, 0
        o.idx = len(self.ops)
        o.force = False
        deps = set()
        for r in reads:
            w = self.last_w.get(r)
            if w is not None:
                deps.add(w)
        for r in writes:
            w = self.last_w.get(r)
            if w is not None:
                deps.add(w)
            for x in self.readers.get(r, ()):
                deps.add(x)
        o.deps = deps
        for r in writes:
            self.last_w[r] = o.idx
            self.readers[r] = []
        for r in reads:
            self.readers.setdefault(r, []).append(o.idx)
        if dma is not None:
            assert self.streams.setdefault(dma, eng) == eng
        self.ops.append(o)
        return o

    def barrier_res(self):
        return list(self.last_w.keys() | self.readers.keys())

    def emit(self):
        nc = self.nc
        ops = self.ops
        for o in ops:
            for d in o.deps:
                ops[d].sig = True
        for o in ops:
            if o.dma is not None:
                o.sig = True
        sems = {}
        for e in self.h:
            sems[e] = self.es.enter_context(nc.semaphore("s_" + e))
        ssems = {}
        for s in self.streams:
            ssems[s] = self.es.enter_context(nc.semaphore("d_" + s))
        cnt = {e: 0 for e in self.h}
        scnt = {s: 0 for s in self.streams}
        seen = {e: {} for e in self.h}
        nwait = 0
        for o in ops:
            h = self.h[o.eng]
            need = {}
            for d in o.deps:
                do = ops[d]
                if do.dma is not None:
                    key = ("s", do.dma)
                    v = do.val
                    if self.streams[do.dma] == o.eng:
                        v = scnt[do.dma]
                else:
                    if do.eng == o.eng and o.eng in ("pe",) and o.dma is None and not o.force:
                        continue
                    key = ("e", do.eng)
                    v = do.val
                if need.get(key, 0) < v:
                    need[key] = v
            if o.dma is not None and scnt[o.dma] > 0:
                need[("s", o.dma)] = max(need.get(("s", o.dma), 0), scnt[o.dma])
            for key, v in need.items():
                if seen[o.eng].get(key, 0) >= v:
                    continue
                seen[o.eng][key] = v
                sem = ssems[key[1]] if key[0] == "s" else sems[key[1]]
                h.wait_ge(sem, v)
                nwait += 1
            ins = o.fn(h)
            if o.dma is not None:
                scnt[o.dma] += 16
                o.val = scnt[o.dma]
                ins.then_inc(ssems[o.dma], 16)
            elif o.sig:
                cnt[o.eng] += 1
                o.val = cnt[o.eng]
                ins.then_inc(sems[o.eng], 1)
        sp = self.h["sp"]
        for s, c in scnt.items():
            if c:
                sp.wait_ge(ssems[s], c)
        for e, c in cnt.items():
            if c and e != "sp":
                sp.wait_ge(sems[e], c)
        return dict(n_ops=len(ops), n_wait=nwait, cnt=cnt)


def _fm(v):
    v = np.asarray(v, np.float32)
    return np.ascontiguousarray(v.reshape(-1, 128).T)


def _kmaj(w):
    return np.ascontiguousarray(w.reshape(8, 128, w.shape[1]).transpose(1, 0, 2))


def _consts():
    k = np.arange(128)
    c = {}
    c["identF"] = np.eye(128, dtype=np.float32)
    c["U"] = (k[:, None] <= k[None, :]).astype(np.float32)
    c["Ms"] = (k[:, None] > k[None, :]).astype(np.float32)
    c["onesF"] = np.ones((128, 128), np.float32)
    c["maskT"] = (k[None, :] >= k[:, None]).astype(np.float32)
    tl = np.arange(512)
    m = np.zeros((128, 4, 512), np.float32)
    for r in range(4):
        m[:, r, :] = np.where(tl[None, :] < r * 128 + k[:, None], NEG, 0.0)
    c["masks"] = m.reshape(128, 2048)
    return np.ascontiguousarray(np.concatenate(
        [c["identF"], c["U"], c["Ms"], c["onesF"], c["maskT"], c["masks"]], axis=1))


C_ID, C_U, C_MS, C_ONE, C_MT, C_MASK = 0, 128, 256, 384, 512, 640
NCONST = 640 + 2048


def build(stage="full", nseg=NSEG, depth=DEPTH, dumps=()):
    nc = bass.Bass("TRN2", target_bir_lowering=False)
    es = ExitStack()
    S = Sched(nc, es)
    dumps = set(dumps)
    dump_specs = {}

    SHAPES = dict(
        xT=[NSEG, 128, 8, T], c_r=[128, 8], consts=[128, NCONST], lnG=[128, 56], lnB=[128, 56],
        adaw=[DEPTH, 18, 128, 8, 512], adab=[DEPTH, 1, 9216],
        fwin=[DEPTH, 2, NF, 128, 8, 256], fwout=[DEPTH, 2, 8, 128, NF, 128],
        wxbc=[DEPTH, 8, 128, 8, 128], wqk=[DEPTH, 8, 128, 8, 64], wsc=[DEPTH, 6, 128, 8, 128],
        wf=[DEPTH, 128, 8, 4], wz=[DEPTH, 128, 8, 512], wdt=[DEPTH, 128, 8, 8], wv=[DEPTH, 128, 8, 256],
        wmo=[DEPTH, 8, 128, 8, 128], cw=[DEPTH, 128, 32], cb=[DEPTH, 128, 8], ssdp=[DEPTH, 128, 24],
        normg=[DEPTH, 128, 512], fb=[DEPTH, 4, 1], scw=[DEPTH, 128, 6])
    declared = {}

    class _Lazy:
        def __init__(self, name):
            self.name = name

        def ap(self):
            if self.name not in declared:
                declared[self.name] = nc.dram_tensor(self.name, list(SHAPES[self.name]), F32, kind="ExternalInput").ap()
            return declared[self.name]

        def __getitem__(self, key):
            return self.ap()[key]

    xT_d, c_d, consts_d, lnG_d, lnB_d = _Lazy("xT"), _Lazy("c_r"), _Lazy("consts"), _Lazy("lnG"), _Lazy("lnB")
    adaw_d, adab_d, fwin_d, fwout_d = _Lazy("adaw"), _Lazy("adab"), _Lazy("fwin"), _Lazy("fwout")
    wxbc_d, wqk_d, wsc_d, wf_d, wz_d, wdt_d, wv_d, wmo_d = (_Lazy(n_) for n_ in ("wxbc", "wqk", "wsc", "wf", "wz", "wdt", "wv", "wmo"))
    cw_d, cb_d, ssdp_d, normg_d, fb_d, scw_d = (_Lazy(n_) for n_ in ("cw", "cb", "ssdp", "normg", "fb", "scw"))
    out_d = nc.dram_tensor("outT", [NSEG, 128, 8, T], F32, kind="ExternalOutput").ap()
    kh_d = nc.dram_tensor("khist", [DEPTH, NSEG, 4, 65, T], BF16).ap()
    vh_d = nc.dram_tensor("vhist", [DEPTH, NSEG, 4, 128, 8, 64], BF16).ap()

    uniq = [0]

    def sb(name, shape, dt=F32, ctx=None):
        uniq[0] += 1
        return (ctx or es).enter_context(nc.sbuf_tensor("s%d_%s" % (uniq[0], name), list(shape), dt))

    def dump(name, ap, shape, dt=F32, res=()):
        if name not in dumps:
            return
        t = nc.dram_tensor("dbg_" + name, list(shape), dt, kind="ExternalOutput").ap()
        S.op("sp", lambda e: e.dma_start(out=t, in_=ap), reads=res, writes=("dbg_" + name,), dma="dbg")
        dump_specs[name] = (shape, dt)

    xT = sb("xT", [128, 8, T])
    hT = sb("hT", [128, 8, T], BF16)
    cst = sb("cst", [128, NCONST])
    identB = sb("identB", [128, 128], BF16)
    masksB = sb("masksB", [128, 4, 512], BF16)
    lnG = sb("lnG", [128, 7, 8])
    lnB = sb("lnB", [128, 7, 8])
    mod = sb("mod", [128, DEPTH, 72])
    agT = sb("agT", [128, 7, 8])
    abT = sb("abT", [128, 7, 8])
    hsT = sb("hsT", [128, 6, 8])
    hbT = sb("hbT", [128, 6, 8])
    gsT = sb("gsT", [128, 6, 8])
    cact = sb("cact", [128, 8])
    cw = sb("cw", [128, DEPTH, 8, 4])
    cbias = sb("cbias", [128, DEPTH, 8])
    ssdp = sb("ssdp", [128, DEPTH, 24])
    Abc = sb("Abc", [128, DEPTH, 8])
    normg = sb("normg", [128, DEPTH, 512])
    nfb = sb("nfb", [4, DEPTH, 1])
    scw = sb("scw", [128, DEPTH, 2, 3])
    ctail = sb("ctail", [128, DEPTH, 8, 3])
    stail = sb("stail", [128, DEPTH, 2, 2])
    fcarry = sb("fcarry", [4, DEPTH, 1])
    kbias = sb("kbias", [128, DEPTH, 32, 4])
    Sst = sb("Sst", [128, DEPTH, 512])
    Sbf = sb("Sbf", [128, DEPTH, 512], BF16)
    ones4 = sb("ones4", [4, T])

    PB = [es.enter_context(nc.psum_tensor("pb%d" % i, [128, 512], F32)) for i in range(7)]
    PT = es.enter_context(nc.psum_tensor("pt", [128, 1024], BF16))

    identF = cst[:, C_ID:C_ID + 128]
    Umat = cst[:, C_U:C_U + 128]
    Msm = cst[:, C_MS:C_MS + 128]
    onesF = cst[:, C_ONE:C_ONE + 128]
    maskT = cst[:, C_MT:C_MT + 128]

    def barrier():
        allres = tuple(S.barrier_res())
        for eng in ("pe", "act", "dve", "pool", "sp"):
            o = S.op(eng, lambda e: e.nop(), reads=(), writes=allres)
            o.force = True

    ldc = [0]

    def ld(dst, src, res, eng="sp"):
        ldc[0] += 1
        S.op(eng, lambda e: e.dma_start(out=dst, in_=src), writes=(res,), dma="setup_%s%d" % (eng, ldc[0] % 6))

    ld(cst[:], consts_d[:, :], "cst")
    ld(lnG[:].rearrange("p a b -> p (a b)"), lnG_d[:, :], "lnG")
    ld(lnB[:].rearrange("p a b -> p (a b)"), lnB_d[:, :], "lnB")
    ld(cact[:], c_d[:, :], "cact")
    for l in range(DEPTH):
        ld(cw[:, l].rearrange("p a b -> p (a b)"), cw_d[l], "cw")
        ld(cbias[:, l], cb_d[l], "cbias")
        ld(ssdp[:, l], ssdp_d[l], "ssdp")
        ld(normg[:, l], normg_d[l], "normg")
        ld(nfb[:, l], fb_d[l], "nfb")
        ld(scw[:, l].rearrange("p a b -> p (a b)"), scw_d[l], "scw")
    ld(identB[:], consts_d[:, C_ID:C_ID + 128], "identB", eng="pool")
    ld(masksB[:].rearrange("p a b -> p (a b)"), consts_d[:, C_MASK:C_MASK + 2048], "masksB", eng="pool")

    S.op("dve", lambda e: e.memset(ctail[:], 0.0), writes=("ctail",))
    S.op("dve", lambda e: e.memset(stail[:], 0.0), writes=("stail",))
    S.op("dve", lambda e: e.memset(fcarry[:], 0.0), writes=("fcarry",))
    S.op("dve", lambda e: e.memset(Sst[:], 0.0), writes=("Sst",))
    S.op("dve", lambda e: e.memset(Sbf[:], 0.0), writes=("Sbf",))
    S.op("dve", lambda e: e.memset(ones4[:], 1.0), writes=("ones4",))
    S.op("act", lambda e: e.activation(out=cact[:], in_=cact[:], func=AF.Silu), reads=("cact",), writes=("cact",))
    S.op("act", lambda e: e.activation(out=Abc[:], in_=ssdp[:, :, 8:16], func=AF.Exp), reads=("ssdp",), writes=("Abc",))
    S.op("dve", lambda e: e.tensor_scalar(out=Abc[:], in0=Abc[:], scalar1=-1.0, scalar2=None, op0=ALU.mult),
         reads=("Abc",), writes=("Abc",))
    S.op("dve", lambda e: e.tensor_scalar(out=nfb[:], in0=nfb[:], scalar1=-1.0, scalar2=None, op0=ALU.mult),
         reads=("nfb",), writes=("nfb",))

    if stage in ("mix", "ln_in"):
        S.op("dve", lambda e: e.memset(mod[:], 0.0), writes=("mod",))
    with ExitStack() as ph:
        aw = [sb("aw%d" % i, [128, 8, 512], F32, ph) for i in range(2)]
        modrow = sb("modrow", [1, 9216], F32, ph)
        adab = sb("adab", [1, 9216], F32, ph)
        for l in range(DEPTH if stage not in ("mix", "ln_in") else 0):
            S.op("sp", lambda e, l=l: e.dma_start(out=adab[:], in_=adab_d[l]), writes=("adab",), dma="adab")
            for ct in range(18):
                a = aw[ct % 2]
                ra = "aw%d" % (ct % 2)
                S.op("sp", lambda e, a=a, l=l, ct=ct: e.dma_start(out=a[:], in_=adaw_d[l, ct]), writes=(ra,), dma=ra)
                for k in range(8):
                    S.op("pe", lambda e, a=a, k=k: e.matmul(PB[0][0:1, :], lhsT=cact[:, k:k + 1], rhs=a[:, k, :],
                                                            start=(k == 0), stop=(k == 7)),
                         reads=(ra, "cact"), writes=("pb0",))
                S.op("dve", lambda e, ct=ct: e.tensor_tensor(out=modrow[0:1, ct * 512:(ct + 1) * 512], in0=PB[0][0:1, :],
                                                             in1=adab[0:1, ct * 512:(ct + 1) * 512], op=ALU.add),
                     reads=("pb0", "adab"), writes=("modrow",))
            for j in range(72):
                S.op("pe", lambda e, j=j: e.matmul(PB[1][:, j:j + 1], lhsT=modrow[0:1, j * 128:(j + 1) * 128],
                                                   rhs=onesF[0:1, 0:1], start=True, stop=True),
                     reads=("modrow", "cst"), writes=("pb1",))
            S.op("dve", lambda e, l=l: e.tensor_copy(out=mod[:, l, :], in_=PB[1][:, 0:72]), reads=("pb1",), writes=("mod",))
        barrier()
    S.op("dve", lambda e: e.tensor_scalar(out=agT[:], in0=lnG[:], scalar1=float(ALPHA), scalar2=None, op0=ALU.mult),
         reads=("lnG",), writes=("agT",))
    S.op("dve", lambda e: e.tensor_scalar(out=abT[:], in0=lnB[:], scalar1=float(ALPHA), scalar2=None, op0=ALU.mult),
         reads=("lnB",), writes=("abT",))
    for n in range(6):
        l, s = n // 3, n % 3
        sh = mod[:, l, s * 24:s * 24 + 8]
        sc = mod[:, l, s * 24 + 8:s * 24 + 16]
        gt = mod[:, l, s * 24 + 16:s * 24 + 24]
        S.op("dve", lambda e, n=n, sc=sc: e.scalar_tensor_tensor(out=hsT[:, n, :], in0=sc, scalar=1.0, in1=lnG[:, n, :],
                                                                 op0=ALU.add, op1=ALU.mult),
             reads=("mod", "lnG"), writes=("hsT",))
        S.op("dve", lambda e, n=n, sc=sc: e.scalar_tensor_tensor(out=hbT[:, n, :], in0=sc, scalar=1.0, in1=lnB[:, n, :],
                                                                 op0=ALU.add, op1=ALU.mult),
             reads=("mod", "lnB"), writes=("hbT",))
        S.op("dve", lambda e, n=n, sh=sh: e.tensor_tensor(out=hbT[:, n, :], in0=hbT[:, n, :], in1=sh, op=ALU.add),
             reads=("mod", "hbT"), writes=("hbT",))
        S.op("dve", lambda e, n=n, gt=gt, s=s: e.tensor_scalar(out=gsT[:, n, :], in0=gt, scalar1=(1.0 if s == 1 else 0.5),
                                                               scalar2=None, op0=ALU.mult),
             reads=("mod",), writes=("gsT",))

    LNB = ([sb("sq%d" % i, [128, 512], F32) for i in range(2)], [sb("lt%d" % i, [128, 512], F32) for i in range(2)],
           sb("mean", [128, 512], F32), sb("var", [128, 512], F32), sb("rstd", [128, 512], F32), sb("nmr", [128, 512], F32),
           sb("xs_", [128, 512], F32), sb("qs_", [128, 512], F32))

    def ln_parts(n, final, tt):
        sq, tmp, mean, var, rstd, nmr, xs_, qs_ = LNB
        ts = slice(tt * 512, (tt + 1) * 512)

        def part_a():
            S.op("pool", lambda e, ts=ts: e.tensor_tensor(out=xs_[:], in0=xT[:, 0, ts], in1=xT[:, 1, ts], op=ALU.add),
                 reads=(("xT", 0, tt), ("xT", 1, tt)), writes=("xs_",))
            for k in range(2, 8):
                S.op("pool", lambda e, k=k, ts=ts: e.tensor_tensor(out=xs_[:], in0=xs_[:], in1=xT[:, k, ts], op=ALU.add),
                     reads=("xs_", ("xT", k, tt)), writes=("xs_",))
            S.op("act", lambda e, ts=ts: e.activation(out=qs_[:], in_=xT[:, 0, ts], func=AF.Square),
                 reads=(("xT", 0, tt),), writes=("qs_",))
            for k in range(1, 8):
                q = sq[k % 2]
                rq = "sq%d" % (k % 2)
                S.op("act", lambda e, q=q, k=k, ts=ts: e.activation(out=q[:], in_=xT[:, k, ts], func=AF.Square),
                     reads=(("xT", k, tt),), writes=(rq,))
                S.op("dve", lambda e, q=q: e.tensor_tensor(out=qs_[:], in0=qs_[:], in1=q[:], op=ALU.add),
                     reads=("qs_", rq), writes=("qs_",))

        def part_b():
            S.op("pe", lambda e: e.matmul(PB[0][:, :], lhsT=onesF, rhs=xs_[:], start=True, stop=True),
                 reads=("cst", "xs_"), writes=("pb0",))
            S.op("pe", lambda e: e.matmul(PB[1][:, :], lhsT=onesF, rhs=qs_[:], start=True, stop=True),
                 reads=("cst", "qs_"), writes=("pb1",))
            S.op("dve", lambda e: e.tensor_scalar(out=mean[:], in0=PB[0][:, :], scalar1=1.0 / D, scalar2=None, op0=ALU.mult),
                 reads=("pb0",), writes=("mean",))
            S.op("dve", lambda e: e.tensor_tensor(out=var[:], in0=mean[:], in1=mean[:], op=ALU.mult),
                 reads=("mean",), writes=("var",))
            S.op("dve", lambda e: e.scalar_tensor_tensor(out=var[:], in0=PB[1][:, :], scalar=1.0 / D, in1=var[:],
                                                         op0=ALU.mult, op1=ALU.subtract),
                 reads=("pb1", "var"), writes=("var",))
            S.op("dve", lambda e: e.tensor_scalar(out=var[:], in0=var[:], scalar1=float(LN_EPS), scalar2=None, op0=ALU.add),
                 reads=("var",), writes=("var",))
            S.op("act", lambda e: e.activation(out=rstd[:], in_=var[:], func=AF.Ln), reads=("var",), writes=("rstd",))
            S.op("act", lambda e: e.activation(out=rstd[:], in_=rstd[:], func=AF.Exp, scale=-0.5),
                 reads=("rstd",), writes=("rstd",))
            S.op("dve", lambda e: e.scalar_tensor_tensor(out=nmr[:], in0=mean[:], scalar=-1.0, in1=rstd[:],
                                                         op0=ALU.mult, op1=ALU.mult),
                 reads=("mean", "rstd"), writes=("nmr",))

        def part_c():
            for k in range(8):
                t_ = tmp[k % 2]
                rt = "lt%d" % (k % 2)
                S.op("dve", lambda e, t_=t_, k=k, ts=ts: e.tensor_tensor(out=t_[:], in0=xT[:, k, ts], in1=rstd[:], op=ALU.mult),
                     reads=(("xT", k, tt), "rstd"), writes=(rt,))
                S.op("dve", lambda e, t_=t_: e.tensor_tensor(out=t_[:], in0=t_[:], in1=nmr[:], op=ALU.add),
                     reads=(rt, "nmr"), writes=(rt,))
                if final:
                    S.op("act", lambda e, t_=t_, k=k, ts=ts: e.activation(out=xT[:, k, ts], in_=t_[:], func=AF.Identity,
                                                                          scale=lnG[:, n, k:k + 1], bias=lnB[:, n, k:k + 1]),
                         reads=(rt, "lnG", "lnB"), writes=(("xT", k, tt),))
                else:
                    S.op("act", lambda e, t_=t_, k=k, ts=ts: e.activation(out=xT[:, k, ts], in_=t_[:], func=AF.Identity,
                                                                          scale=agT[:, n, k:k + 1], bias=abT[:, n, k:k + 1]),
                         reads=(rt, "agT", "abT"), writes=(("xT", k, tt),))
                    S.op("act", lambda e, t_=t_, k=k, ts=ts: e.activation(out=hT[:, k, ts], in_=t_[:], func=AF.Identity,
                                                                          scale=hsT[:, n, k:k + 1], bias=hbT[:, n, k:k + 1]),
                         reads=(rt, "hsT", "hbT"), writes=(("hT", k, tt),))


        return part_a, part_b, part_c

    def layer_norm(n, final=False, only=None):
        for tt in range(T // 512):
            if only is not None and tt != only:
                continue
            for p_ in ln_parts(n, final, tt):
                p_()

    XT_ALL = [("xT", k, tt) for k in range(8) for tt in range(T // 512)]
    HT_ALL = [("hT", k, tt) for k in range(8) for tt in range(T // 512)]

    def ffn(l, which, n, bar, nxt):
        with ExitStack() as ph:
            actT = sb("actT", [128, NF, T], BF16, ph)
            wio = [sb("wio%d" % i, [128, 8, 256], BF16, ph) for i in range(3)]
            wo = [sb("wo%d" % i, [128, NF, 128], BF16, ph) for i in range(3)]
            sg = [sb("sg%d" % i, [128, 512], F32, ph) for i in range(2)]
            it = 0
            order = [(f, 0) for f in range(3)] + [(f, 1) for f in range(3)] + [(f, tt) for f in range(3, NF) for tt in range(T // 512)]
            loaded = set()
            for f, tt in order:
                w = wio[f % 3]
                rw = "wio%d" % (f % 3)
                if f not in loaded:
                    loaded.add(f)
                    S.op("pool", lambda e, w=w, f=f: e.dma_start(out=w[:], in_=fwin_d[l, which, f]), writes=(rw,), dma=rw)
                if True:
                    ts = slice(tt * 512, (tt + 1) * 512)
                    bg, bu = PB[it % 2], PB[2 + it % 2]
                    rg, ru = "pb%d" % (it % 2), "pb%d" % (2 + it % 2)
                    s_ = sg[it % 2]
                    rs = "sg%d" % (it % 2)
                    it += 1
                    for k in range(8):
                        S.op("pe", lambda e, w=w, k=k, ts=ts, bg=bg: e.matmul(bg[:, :], lhsT=w[:, k, 0:128], rhs=hT[:, k, ts],
                                                                              start=(k == 0), stop=(k == 7)),
                             reads=(rw, ("hT", k, tt)), writes=(rg,))
                    for k in range(8):
                        S.op("pe", lambda e, w=w, k=k, ts=ts, bu=bu: e.matmul(bu[:, :], lhsT=w[:, k, 128:256], rhs=hT[:, k, ts],
                                                                              start=(k == 0), stop=(k == 7)),
                             reads=(rw, ("hT", k, tt)), writes=(ru,))
                    S.op("act", lambda e, s_=s_, bg=bg: e.activation(out=s_[:], in_=bg[:, :], func=AF.Silu),
                         reads=(rg,), writes=(rs,))
                    S.op("dve", lambda e, s_=s_, bu=bu, f=f, ts=ts: e.tensor_tensor(out=actT[:, f, ts], in0=s_[:], in1=bu[:, :], op=ALU.mult),
                         reads=(rs, ru), writes=(("actT", f, tt),))
            it = 0
            lnp = ln_parts(nxt[0], nxt[1], 0)
            for tt in range(T // 512):
                ts = slice(tt * 512, (tt + 1) * 512)
                for m in range(8):
                    w = wo[it % 3]
                    rw = "wo%d" % (it % 3)
                    S.op("pool", lambda e, w=w, m=m: e.dma_start(out=w[:], in_=fwout_d[l, which, m]), writes=(rw,), dma=rw)
                    by = PB[4 + it % 2]
                    ry = "pb%d" % (4 + it % 2)
                    it += 1
                    for f in range(NF):
                        S.op("pe", lambda e, w=w, f=f, ts=ts, by=by: e.matmul(by[:, :], lhsT=w[:, f, :], rhs=actT[:, f, ts],
                                                                              start=(f == 0), stop=(f == NF - 1)),
                             reads=(rw, ("actT", f, tt)), writes=(ry,))
                    S.op("dve", lambda e, m=m, ts=ts, by=by: e.scalar_tensor_tensor(out=xT[:, m, ts], in0=by[:, :], scalar=gsT[:, n, m:m + 1],
                                                                                    in1=xT[:, m, ts], op0=ALU.mult, op1=ALU.add),
                         reads=(ry, "gsT", ("xT", m, tt)), writes=(("xT", m, tt),))
                    if tt == 1 and m == 0:
                        lnp[0]()
                    if tt == 1 and m == 3:
                        lnp[1]()
                    if tt == 1 and m == 4:
                        lnp[2]()
            if bar:
                barrier()

    def mixer(l, seg, n):
        NB = T // 128
        NT = T // 512
        pc = [0]

        def proj_fm(w, rw, ncol, dst_fn, evac_eng="act"):
            for tt in range(NT):
                ts = slice(tt * 512, (tt + 1) * 512)
                bank = PB[pc[0] % 2]
                rb = "pb%d" % (pc[0] % 2)
                pc[0] += 1
                for k in range(8):
                    S.op("pe", lambda e, k=k, ts=ts, bank=bank: e.matmul(bank[0:ncol, :], lhsT=w[:, k, 0:ncol], rhs=hT[:, k, ts],
                                                                         start=(k == 0), stop=(k == 7)),
                         reads=(rw, ("hT", k, tt)), writes=(rb,))
                dst_fn(tt, ts, bank, rb)

        def wload(w, rw, src):
            S.op("pool", lambda e: e.dma_start(out=w[:], in_=src), writes=(rw,), dma=rw)

        with ExitStack() as mx:
            xbcT = sb("xbcT", [128, 8, T], BF16, mx)
            QT = sb("QT", [128, 4, T], BF16, mx)
            KT = sb("KT", [128, 4, T], BF16, mx)
            Vo = sb("Vo", [128, NB, 4, 64], BF16, mx)
            yT = sb("yT", [128, 8, T], BF16, mx)
            with ExitStack() as ph:
                stg = [sb("stg%d" % i, [128, 3 + T], F32, ph) for i in range(2)]
                acc = [sb("acc%d" % i, [128, T], F32, ph) for i in range(2)]
                wx = [sb("wx%d" % i, [128, 8, 128], BF16, ph) for i in range(3)]
                wq = [sb("wq%d" % i, [128, 8, 64], BF16, ph) for i in range(3)]
                wfs = sb("wfs", [128, 8, 4], BF16, ph)
                wvs = sb("wvs", [128, 8, 256], BF16, ph)
                fT = sb("fT", [4, T], F32, ph)
                cum = sb("cum", [4, T], F32, ph)
                crow = sb("crow", [4, T], BF16, ph)
                scB = sb("scB", [128, T], F32, ph)
                scC = sb("scC", [128, T], F32, ph)
                scU = sb("scU", [128, 2 + T], F32, ph)
                scA = sb("scA", [128, T], F32, ph)
                for i in range(8):
                    w, rw = wx[i % 3], "wx%d" % (i % 3)
                    wload(w, rw, wxbc_d[l, i])
                    st, rst = stg[i % 2], "stg%d" % (i % 2)
                    a, ra = acc[i % 2], "acc%d" % (i % 2)
                    S.op("dve", lambda e, st=st, i=i: e.tensor_copy(out=st[:, 0:3], in_=ctail[:, l, i, :]),
                         reads=("ctail",), writes=(rst,))

                    def ev(tt, ts, bank, rb, st=st, rst=rst):
                        S.op("act", lambda e: e.activation(out=st[:, 3 + tt * 512:3 + (tt + 1) * 512], in_=bank[:, :], func=AF.Identity),
                             reads=(rb,), writes=(rst,))
                    proj_fm(w, rw, 128, ev)
                    S.op("dve", lambda e, st=st, a=a, i=i: e.tensor_scalar(out=a[:], in0=st[:, 0:T], scalar1=cw[:, l, i, 0:1], scalar2=None, op0=ALU.mult),
                         reads=(rst, "cw"), writes=(ra,))
                    for j in range(1, 4):
                        S.op("dve", lambda e, st=st, a=a, i=i, j=j: e.scalar_tensor_tensor(out=a[:], in0=st[:, j:j + T], scalar=cw[:, l, i, j:j + 1],
                                                                                           in1=a[:], op0=ALU.mult, op1=ALU.add),
                             reads=(rst, "cw", ra), writes=(ra,))
                    S.op("dve", lambda e, st=st, i=i: e.tensor_copy(out=ctail[:, l, i, :], in_=st[:, T:T + 3]),
                         reads=(rst,), writes=("ctail",))
                    S.op("act", lambda e, a=a, i=i: e.activation(out=xbcT[:, i, :], in_=a[:], func=AF.Silu, bias=cbias[:, l, i:i + 1]),
                         reads=(ra, "cbias"), writes=(("xbcT", i),))
                for j in range(8):
                    w, rw = wq[j % 3], "wq%d" % (j % 3)
                    wload(w, rw, wqk_d[l, j])
                    dstT, rd = (QT, "QT") if j < 4 else (KT, "KT")

                    def ev(tt, ts, bank, rb, dstT=dstT, rd=rd, j=j):
                        S.op("act", lambda e: e.activation(out=dstT[0:64, j % 4, ts], in_=bank[0:64, :], func=AF.Identity),
                             reads=(rb,), writes=((rd, j % 4),))
                    proj_fm(w, rw, 64, ev)
                S.op("dve", lambda e: e.memset(KT[64:65, :, :], 1.0), writes=tuple(("KT", h) for h in range(4)))
                wload(wfs, "wfs", wf_d[l])

                def ev(tt, ts, bank, rb):
                    S.op("act", lambda e: e.activation(out=fT[:, ts], in_=bank[0:4, :], func=AF.Exp, scale=-1.0, bias=nfb[:, l, :]),
                         reads=(rb, "nfb"), writes=("fT",))
                proj_fm(wfs, "wfs", 4, ev)
                S.op("act", lambda e: e.activation(out=fT[:], in_=fT[:], func=AF.Ln, bias=1.0), reads=("fT",), writes=("fT",))
                S.op("dve", lambda e: e.tensor_tensor_scan(out=cum[:], data0=ones4[:], data1=fT[:], initial=fcarry[:, l, :],
                                                           op0=ALU.mult, op1=ALU.add),
                     reads=("fT", "ones4", "fcarry"), writes=("cum",))
                S.op("dve", lambda e: e.tensor_copy(out=fcarry[:, l, :], in_=cum[:, T - 1:T]), reads=("cum",), writes=("fcarry",))
                S.op("act", lambda e: e.activation(out=crow[:], in_=cum[:], func=AF.Identity, scale=-8.0), reads=("cum",), writes=("crow",))
                for h in range(4):
                    S.op("sp", lambda e, h=h: e.dma_start(out=QT[64:65, h, :], in_=crow[h:h + 1, :]),
                         reads=("crow",), writes=(("QT", h),), dma="crow%d" % h)
                for blk in range(NB):
                    S.op("pe", lambda e, blk=blk: e.transpose(out=PB[5][:, blk * 4:(blk + 1) * 4], in_=cum[0:4, blk * 128:(blk + 1) * 128],
                                                              identity=identF[0:4, 0:4]),
                         reads=("cum", "cst"), writes=("pb5",))
                S.op("dve", lambda e: e.tensor_copy(out=kbias[:, l, seg * NB:(seg + 1) * NB, :],
                                                    in_=PB[5][:, 0:4 * NB].rearrange("p (a b) -> p a b", b=4)),
                     reads=("pb5",), writes=("kbias",))
                wload(wvs, "wvs", wv_d[l])
                for blk in range(NB):
                    bank, rb = PB[2 + blk % 2], "pb%d" % (2 + blk % 2)
                    tsl = slice(blk * 128, (blk + 1) * 128)
                    for k in range(8):
                        S.op("pe", lambda e, k=k, tsl=tsl, bank=bank: e.matmul(bank[:, 0:256], lhsT=hT[:, k, tsl], rhs=wvs[:, k, :],
                                                                               start=(k == 0), stop=(k == 7)),
                             reads=("wvs", ("hT", k, blk // 4)), writes=(rb,))
                    S.op("act", lambda e, blk=blk, bank=bank: e.activation(out=Vo[:, blk, :, :].rearrange("p a b -> p (a b)"), in_=bank[:, 0:256],
                                                                           func=AF.Identity),
                         reads=(rb,), writes=("Vo",))
                for i in range(2):
                    for which, dst, rd in ((0, scB, "scB"), (1, scC, "scC")):
                        w, rw = wx[pc[0] % 3], "wx%d" % (pc[0] % 3)
                        wload(w, rw, wsc_d[l, which * 2 + i])

                        def ev(tt, ts, bank, rb, dst=dst, rd=rd):
                            S.op("act", lambda e: e.activation(out=dst[:, ts], in_=bank[:, :], func=AF.Identity),
                                 reads=(rb,), writes=(rd,))
                        proj_fm(w, rw, 128, ev)
                    S.op("dve", lambda e, i=i: e.tensor_copy(out=scU[:, 0:2], in_=stail[:, l, i, :]), reads=("stail",), writes=("scU",))
                    w, rw = wx[pc[0] % 3], "wx%d" % (pc[0] % 3)
                    wload(w, rw, wsc_d[l, 4 + i])

                    def ev(tt, ts, bank, rb):
                        S.op("dve", lambda e: e.tensor_tensor(out=scU[:, 2 + tt * 512:2 + (tt + 1) * 512], in0=bank[:, :], in1=scC[:, ts], op=ALU.mult),
                             reads=(rb, "scC"), writes=("scU",))
                    proj_fm(w, rw, 128, ev)
                    S.op("dve", lambda e, i=i: e.tensor_scalar(out=scA[:], in0=scU[:, 0:T], scalar1=scw[:, l, i, 0:1], scalar2=None, op0=ALU.mult),
                         reads=("scU", "scw"), writes=("scA",))
                    for j in range(1, 3):
                        S.op("dve", lambda e, i=i, j=j: e.scalar_tensor_tensor(out=scA[:], in0=scU[:, j:j + T], scalar=scw[:, l, i, j:j + 1],
                                                                               in1=scA[:], op0=ALU.mult, op1=ALU.add),
                             reads=("scU", "scw", "scA"), writes=("scA",))
                    S.op("dve", lambda e, i=i: e.tensor_copy(out=stail[:, l, i, :], in_=scU[:, T:T + 2]), reads=("scU",), writes=("stail",))
                    S.op("dve", lambda e, i=i: e.tensor_tensor(out=yT[:, 6 + i, :], in0=scA[:], in1=scB[:], op=ALU.mult),
                         reads=("scA", "scB"), writes=(("yT", 6 + i),))
                for h in range(4):
                    S.op("sp", lambda e, h=h: e.dma_start(out=kh_d[l, seg, h], in_=KT[0:65, h, :]),
                         reads=(("KT", h),), writes=(("kh", l, seg, h),), dma="hwk%d" % h)
                    S.op("sp", lambda e, h=h: e.dma_start(out=vh_d[l, seg, h], in_=Vo[:, :, h, :]),
                         reads=("Vo",), writes=(("vh", l, seg, h),), dma="hwv%d" % h)
                barrier()
            with ExitStack() as ph:
                Kb = [sb("Kb%d" % i, [128, T], BF16, ph) for i in range(2)]
                Vb = [sb("Vb%d" % i, [128, NB, 2, 64], BF16, ph) for i in range(2)]
                Pb = [sb("Pb%d" % i, [128, 512], BF16, ph) for i in range(3)]
                rec = sb("rec", [128, 512], F32, ph)
                for i in range(2):
                    S.op("dve", lambda e, i=i: e.memset(Vb[i][:, :, 1, :], 1.0), writes=("Vb%d" % i,))
                li = 0
                items = []
                for h in range(4):
                    ob = (0, 1) if h % 2 == 0 else (5, 6)
                    for ks in range(seg + 1):
                        slot = li % 2
                        li += 1
                        newload = True
                        for blk in range(NB):
                            for q in range(NT):
                                if ks == seg and blk >= 4 * (q + 1):
                                    continue
                                items.append(dict(h=h, ks=ks, blk=blk, q=q, slot=slot, load=newload, ob=ob[q],
                                                  diag=(ks == seg and blk >= 4 * q), first=(ks == 0 and blk == 0),
                                                  last=(ks == seg and blk == 4 * q + 3)))
                                newload = False

                def emit_s(i, it):
                    h, ks, blk, q, slot = it["h"], it["ks"], it["blk"], it["q"], it["slot"]
                    kb_, vb_ = Kb[slot], Vb[slot]
                    rk, rv = "Kb%d" % slot, "Vb%d" % slot
                    if it["load"]:
                        S.op("sp", lambda e: e.dma_start(out=kb_[0:65, :], in_=kh_d[l, ks, h]),
                             reads=(("kh", l, ks, h),), writes=(rk,), dma=rk)
                        S.op("sp", lambda e: e.dma_start(out=vb_[:, :, 0, :], in_=vh_d[l, ks, h]),
                             reads=(("vh", l, ks, h),), writes=(rv,), dma=rv)
                    Sb, rs = PB[2 + i % 3], "pb%d" % (2 + i % 3)
                    p_, rp = Pb[i % 3], "Pb%d" % (i % 3)
                    diag = it["diag"]
                    S.op("pe", lambda e: e.matmul(Sb[:, :], lhsT=kb_[0:65, blk * 128:(blk + 1) * 128], rhs=QT[0:65, h, q * 512:(q + 1) * 512],
                                                  start=True, stop=(not diag)), reads=(rk, ("QT", h)), writes=(rs,))
                    if diag:
                        S.op("pe", lambda e: e.matmul(Sb[:, :], lhsT=identB[:], rhs=masksB[:, blk - 4 * q, :], start=False, stop=True),
                             reads=("identB", "masksB"), writes=(rs,))
                    S.op("act", lambda e: e.activation(out=p_[:], in_=Sb[:, :], func=AF.Exp, scale=0.125, bias=kbias[:, l, ks * NB + blk, h:h + 1]),
                         reads=(rs, "kbias"), writes=(rp,))

                def emit_pv(i, it):
                    h, blk, q, slot = it["h"], it["blk"], it["q"], it["slot"]
                    vb_, rv = Vb[slot], "Vb%d" % slot
                    p_, rp = Pb[i % 3], "Pb%d" % (i % 3)
                    O_ = PB[it["ob"]]
                    ro = "pb%d" % it["ob"]
                    S.op("pe", lambda e: e.matmul(O_[:, :], lhsT=vb_[:, blk, :, :].rearrange("p a b -> p (a b)"), rhs=p_[:],
                                                  start=it["first"], stop=it["last"]), reads=(rv, rp), writes=(ro,))
                    if it["last"]:
                        S.op("dve", lambda e: e.reciprocal(out=rec[64:128, :], in_=O_[64:128, :]), reads=(ro,), writes=("rec",))
                        S.op("dve", lambda e: e.tensor_tensor(out=yT[(h % 2) * 64:(h % 2) * 64 + 64, 4 + h // 2, q * 512:(q + 1) * 512],
                                                              in0=O_[0:64, :], in1=rec[64:128, :], op=ALU.mult),
                             reads=(ro, "rec"), writes=(("yT", 4 + h // 2),))

                LOOK = 2
                for i in range(len(items) + LOOK):
                    if i < len(items):
                        emit_s(i, items[i])
                    if i >= LOOK:
                        emit_pv(i - LOOK, items[i - LOOK])
                barrier()
            with ExitStack() as ph:
                wzs = sb("wzs", [128, 8, 512], BF16, ph)
                wdts = sb("wdts", [128, 8, 8], BF16, ph)
                wload(wzs, "wzs", wz_d[l])
                wload(wdts, "wdts", wdt_d[l])
                dtr = sb("dtr", [128, 8], F32, ph)
                dt_ = sb("dt_", [128, 8], F32, ph)
                a_ = sb("a_", [128, 8], F32, ph)
                acs = sb("acs", [128, 8], F32, ph)
                ea = sb("ea", [128, 8], F32, ph)
                etot = sb("etot", [128, 8], F32, ph)
                dend = sb("dend", [128, 8], F32, ph)
                aU = sb("aU", [128, 8, 128], F32, ph)
                LT = [sb("LT%d" % g, [128, 4, 128], F32, ph) for g in range(2)]
                MT = [sb("MT%d" % g, [128, 4, 128], BF16, ph) for g in range(2)]
                xsB = sb("xsB", [128, 768], BF16, ph)
                cbm = sb("cbm", [128, 2, 128], F32, ph)
                xdt = sb("xdt", [128, 8, 64], BF16, ph)
                xdd = sb("xdd", [128, 8, 64], BF16, ph)
                y1 = sb("y1", [128, 8, 64], F32, ph)
                y2 = sb("y2", [128, 8, 64], F32, ph)
                sz = sb("sz", [128, 512], F32, ph)
                yz = sb("yz", [128, 512], F32, ph)
                junk = sb("junk", [128, 512], F32, ph)
                ss = sb("ss", [128, 2], F32, ph)
                rt = sb("rt", [128, 2], F32, ph)
                yn = sb("yn", [128, 512], BF16, ph)
                Sv = Sst[:, l, :].rearrange("p (h d) -> p h d", d=64)
                ea2 = [ea, sb("ea_b", [128, 8], F32, ph)]
                etot2 = [etot, sb("etot_b", [128, 8], F32, ph)]
                MT2 = [MT, [sb("MTb%d" % g, [128, 4, 128], BF16, ph) for g in range(2)]]
                xsB2 = [xsB, sb("xsB_b", [128, 768], BF16, ph)]
                xdt2 = [xdt, sb("xdt_b", [128, 8, 64], BF16, ph)]
                xdd2 = [xdd, sb("xdd_b", [128, 8, 64], BF16, ph)]

                def bc(ap2, n_):
                    return ap2.unsqueeze(2).to_broadcast([128, ap2.shape[1], n_])

                def prep(c):
                    pb = c % 2
                    sfx = "_%d" % pb
                    ea_, etot_, MT_, xsB_, xdt_, xdd_ = ea2[pb], etot2[pb], MT2[pb], xsB2[pb], xdt2[pb], xdd2[pb]
                    tsl = slice(c * 128, (c + 1) * 128)
                    tt = c // 4
                    for k in range(8):
                        S.op("pe", lambda e, k=k: e.matmul(PB[0][:, 0:8], lhsT=hT[:, k, tsl], rhs=wdts[:, k, :], start=(k == 0), stop=(k == 7)),
                             reads=("wdts", ("hT", k, tt)), writes=("pb0",))
                    yield
                    S.op("dve", lambda e: e.tensor_tensor(out=dtr[:], in0=PB[0][:, 0:8], in1=ssdp[:, l, 0:8], op=ALU.add),
                         reads=("pb0", "ssdp"), writes=("dtr",))
                    yield
                    S.op("act", lambda e: e.activation(out=dtr[:], in_=dtr[:], func=AF.Exp), reads=("dtr",), writes=("dtr",))
                    S.op("act", lambda e: e.activation(out=dt_[:], in_=dtr[:], func=AF.Ln, bias=1.0), reads=("dtr",), writes=("dt_",))
                    yield
                    S.op("dve", lambda e: e.tensor_tensor(out=a_[:], in0=dt_[:], in1=Abc[:, l, :], op=ALU.mult), reads=("dt_", "Abc"), writes=("a_",))
                    yield
                    S.op("pe", lambda e: e.matmul(PB[0][:, 8:16], lhsT=Umat, rhs=a_[:], start=True, stop=True), reads=("cst", "a_"), writes=("pb0",))
                    S.op("pe", lambda e: e.matmul(PB[0][:, 16:24], lhsT=onesF, rhs=a_[:], start=True, stop=True), reads=("cst", "a_"), writes=("pb0",))
                    yield
                    S.op("act", lambda e: e.activation(out=ea_[:], in_=PB[0][:, 8:16], func=AF.Exp), reads=("pb0",), writes=("ea" + sfx,))
                    S.op("act", lambda e: e.activation(out=etot_[:], in_=PB[0][:, 16:24], func=AF.Exp), reads=("pb0",), writes=("etot" + sfx,))
                    S.op("dve", lambda e: e.tensor_copy(out=acs[:], in_=PB[0][:, 8:16]), reads=("pb0",), writes=("acs",))
                    yield
                    S.op("dve", lambda e: e.tensor_tensor(out=dend[:], in0=PB[0][:, 16:24], in1=acs[:], op=ALU.subtract),
                         reads=("pb0", "acs"), writes=("dend",))
                    yield
                    S.op("act", lambda e: e.activation(out=dend[:], in_=dend[:], func=AF.Exp), reads=("dend",), writes=("dend",))
                    S.op("dve", lambda e: e.tensor_tensor(out=aU[:], in0=Umat.unsqueeze(1).to_broadcast([128, 8, 128]), in1=bc(a_[:], 128), op=ALU.mult),
                         reads=("cst", "a_"), writes=("aU",))
                    yield
                    for i in range(6):
                        S.op("pe", lambda e, i=i: e.transpose(out=PT[:, i * 128:(i + 1) * 128], in_=xbcT[:, i, tsl], identity=identB[:]),
                             reads=(("xbcT", i), "identB"), writes=("pt",))
                    yield
                    S.op("dve", lambda e: e.tensor_copy(out=xsB_[:], in_=PT[:, 0:768]), reads=("pt",), writes=("xsB" + sfx,))
                    for g in range(2):
                        S.op("pe", lambda e, g=g: e.matmul(PB[0][:, 128 + g * 128:128 + (g + 1) * 128], lhsT=xbcT[:, 4 + g, tsl], rhs=xbcT[:, 6 + g, tsl],
                                                           start=True, stop=True),
                             reads=(("xbcT", 4 + g), ("xbcT", 6 + g)), writes=("pb0e",))
                    yield
                    S.op("dve", lambda e: e.tensor_tensor(out=cbm[:], in0=PB[0][:, 128:384].rearrange("p (a b) -> p a b", b=128),
                                                          in1=maskT.unsqueeze(1).to_broadcast([128, 2, 128]), op=ALU.mult),
                         reads=("pb0e", "cst"), writes=("cbm",))
                    yield
                    for g in range(2):
                        for e4 in range(4):
                            S.op("pe", lambda e, g=g, e4=e4: e.matmul(PB[1 + g][:, e4 * 128:(e4 + 1) * 128], lhsT=Msm, rhs=aU[:, 4 * g + e4, :],
                                                                       start=True, stop=True),
                                 reads=("cst", "aU"), writes=("pb%d" % (1 + g),))
                        yield
                        S.op("act", lambda e, g=g: e.activation(out=LT[g][:].rearrange("p a b -> p (a b)"), in_=PB[1 + g][:, :], func=AF.Exp),
                             reads=("pb%d" % (1 + g),), writes=("LT%d" % g,))
                        yield
                    xs3 = xsB_[:, 0:512].rearrange("p (h d) -> p h d", d=64)
                    S.op("dve", lambda e: e.tensor_tensor(out=xdt_[:], in0=xs3, in1=bc(dt_[:], 64), op=ALU.mult),
                         reads=("xsB" + sfx, "dt_"), writes=("xdt" + sfx,))
                    yield
                    S.op("dve", lambda e: e.tensor_tensor(out=xdd_[:], in0=xdt_[:], in1=bc(dend[:], 64), op=ALU.mult),
                         reads=("xdt" + sfx, "dend"), writes=("xdd" + sfx,))
                    yield
                    for g in range(2):
                        S.op("dve", lambda e, g=g: e.tensor_tensor(out=MT_[g][:], in0=LT[g][:], in1=cbm[:, g:g + 1, :].to_broadcast([128, 4, 128]), op=ALU.mult),
                             reads=("LT%d" % g, "cbm"), writes=("MT%d%s" % (g, sfx),))
                        yield

                def post(c):
                    pb = c % 2
                    sfx = "_%d" % pb
                    ea_, etot_, MT_, xsB_, xdt_, xdd_ = ea2[pb], etot2[pb], MT2[pb], xsB2[pb], xdt2[pb], xdd2[pb]
                    tsl = slice(c * 128, (c + 1) * 128)
                    tt = c // 4
                    xs3 = xsB_[:, 0:512].rearrange("p (h d) -> p h d", d=64)
                    for h in range(8):
                        S.op("pe", lambda e, h=h: e.matmul(PB[4][:, h * 64:(h + 1) * 64], lhsT=MT_[h // 4][:, h % 4, :], rhs=xdt_[:, h, :],
                                                           start=True, stop=True),
                             reads=("MT%d%s" % (h // 4, sfx), "xdt" + sfx), writes=("pb4",))
                    for g in range(2):
                        S.op("pe", lambda e, g=g: e.matmul(PB[5][:, g * 256:(g + 1) * 256], lhsT=xbcT[:, 6 + g, tsl],
                                                           rhs=Sbf[:, l, g * 256:(g + 1) * 256], start=True, stop=True),
                             reads=(("xbcT", 6 + g), "Sbf"), writes=("pb5",))
                    for g in range(2):
                        S.op("pe", lambda e, g=g: e.matmul(PB[6][:, g * 256:(g + 1) * 256], lhsT=xsB_[:, 512 + g * 128:512 + (g + 1) * 128],
                                                           rhs=xdd_[:, 4 * g:4 * g + 4, :].rearrange("p a b -> p (a b)"), start=True, stop=True),
                             reads=("xsB" + sfx, "xdd" + sfx), writes=("pb6",))
                    yield
                    S.op("dve", lambda e: e.tensor_tensor(out=y1[:], in0=PB[5][:, :].rearrange("p (h d) -> p h d", d=64), in1=bc(ea_[:], 64), op=ALU.mult),
                         reads=("pb5", "ea" + sfx), writes=("y1",))
                    yield
                    S.op("dve", lambda e: e.tensor_tensor(out=Sv, in0=Sv, in1=bc(etot_[:], 64), op=ALU.mult), reads=("Sst", "etot" + sfx), writes=("Sst",))
                    yield
                    S.op("dve", lambda e: e.tensor_tensor(out=Sst[:, l, :], in0=Sst[:, l, :], in1=PB[6][:, :], op=ALU.add),
                         reads=("Sst", "pb6"), writes=("Sst",))
                    yield
                    S.op("act", lambda e: e.activation(out=Sbf[:, l, :], in_=Sst[:, l, :], func=AF.Identity), reads=("Sst",), writes=("Sbf",))
                    S.op("dve", lambda e: e.tensor_tensor(out=y1[:], in0=y1[:], in1=PB[4][:, :].rearrange("p (h d) -> p h d", d=64), op=ALU.add),
                         reads=("pb4", "y1"), writes=("y1",))
                    yield
                    S.op("dve", lambda e: e.tensor_tensor(out=y2[:], in0=xs3, in1=bc(ssdp[:, l, 16:24], 64), op=ALU.mult),
                         reads=("xsB" + sfx, "ssdp"), writes=("y2",))
                    for k in range(8):
                        S.op("pe", lambda e, k=k: e.matmul(PB[3][:, :], lhsT=hT[:, k, tsl], rhs=wzs[:, k, :], start=(k == 0), stop=(k == 7)),
                             reads=("wzs", ("hT", k, tt)), writes=("pb3",))
                    yield
                    S.op("dve", lambda e: e.tensor_tensor(out=y1[:], in0=y1[:], in1=y2[:], op=ALU.add), reads=("y1", "y2"), writes=("y1",))
                    S.op("act", lambda e: e.activation(out=sz[:], in_=PB[3][:, :], func=AF.Silu), reads=("pb3",), writes=("sz",))
                    yield
                    S.op("dve", lambda e: e.tensor_tensor(out=yz[:], in0=y1[:].rearrange("p a b -> p (a b)"), in1=sz[:], op=ALU.mult),
                         reads=("y1", "sz"), writes=("yz",))
                    yield
                    for g in range(2):
                        S.op("act", lambda e, g=g: e.activation(out=junk[:, g * 256:(g + 1) * 256], in_=yz[:, g * 256:(g + 1) * 256], func=AF.Square,
                                                                accum_out=ss[:, g:g + 1]),
                             reads=("yz",), writes=("junk", "ss"))
                    yield
                    S.op("dve", lambda e: e.tensor_scalar(out=rt[:], in0=ss[:], scalar1=1.0 / 256, scalar2=float(RMS_EPS), op0=ALU.mult, op1=ALU.add),
                         reads=("ss",), writes=("rt",))
                    yield
                    S.op("act", lambda e: e.activation(out=rt[:], in_=rt[:], func=AF.Ln), reads=("rt",), writes=("rt",))
                    S.op("act", lambda e: e.activation(out=rt[:], in_=rt[:], func=AF.Exp, scale=-0.5), reads=("rt",), writes=("rt",))
                    yield
                    for g in range(2):
                        S.op("dve", lambda e, g=g: e.scalar_tensor_tensor(out=yn[:, g * 256:(g + 1) * 256], in0=yz[:, g * 256:(g + 1) * 256],
                                                                          scalar=rt[:, g:g + 1], in1=normg[:, l, g * 256:(g + 1) * 256],
                                                                          op0=ALU.mult, op1=ALU.mult),
                             reads=("yz", "rt", "normg"), writes=("yn",))
                        yield
                    for half in range(2):
                        for i in range(2):
                            S.op("pe", lambda e, i=i, half=half: e.transpose(out=PT[:, 768 + i * 128:768 + (i + 1) * 128],
                                                                            in_=yn[:, (2 * half + i) * 128:(2 * half + i + 1) * 128], identity=identB[:]),
                                 reads=("yn", "identB"), writes=("pt2",))
                        yield
                        S.op("dve", lambda e, half=half: e.tensor_copy(out=yT[:, 2 * half:2 * half + 2, tsl],
                                                                      in_=PT[:, 768:1024].rearrange("p (a b) -> p a b", b=128)),
                             reads=("pt2",), writes=(("yT", 2 * half), ("yT", 2 * half + 1)))
                        yield

                def run_il(gens):
                    gens = [g for g in gens if g is not None]
                    while gens:
                        for g in list(gens):
                            try:
                                next(g)
                            except StopIteration:
                                gens.remove(g)

                run_il([prep(0)])
                for c in range(NB):
                    run_il([post(c), prep(c + 1) if c + 1 < NB else None])
                barrier()
            if stage == "mix":
                dump("yT%d" % seg, yT[:], [128, 8, T], BF16, res=tuple(("yT", i) for i in range(8)))
            with ExitStack() as ph:
                wm = [sb("wm%d" % i, [128, 8, 128], BF16, ph) for i in range(3)]
                it = 0
                for m in range(8):
                    w, rw = wm[m % 3], "wm%d" % (m % 3)
                    wload(w, rw, wmo_d[l, m])
                    for tt in range(NT):
                        ts = slice(tt * 512, (tt + 1) * 512)
                        bank, rb = PB[it % 2], "pb%d" % (it % 2)
                        it += 1
                        for k in range(8):
                            S.op("pe", lambda e, w=w, k=k, ts=ts, bank=bank: e.matmul(bank[:, :], lhsT=w[:, k, :], rhs=yT[:, k, ts],
                                                                                      start=(k == 0), stop=(k == 7)),
                                 reads=(rw, ("yT", k)), writes=(rb,))
                        S.op("dve", lambda e, m=m, ts=ts, bank=bank: e.scalar_tensor_tensor(out=xT[:, m, ts], in0=bank[:, :], scalar=gsT[:, n, m:m + 1],
                                                                                            in1=xT[:, m, ts], op0=ALU.mult, op1=ALU.add),
                             reads=(rb, "gsT", ("xT", m, tt)), writes=(("xT", m, tt),))
                barrier()

    for seg in range(nseg):
        for k in range(8):
            S.op("sp", lambda e, k=k, seg=seg: e.dma_start(out=xT[:, k, :], in_=xT_d[seg, :, k, :]),
                 writes=(("xT", k, 0), ("xT", k, 1)), dma="xin%d" % k)
        layer_norm(0)
        if stage == "ln_in":
            dump("xT", xT[:], [128, 8, T], res=XT_ALL)
            dump("hT", hT[:], [128, 8, T], BF16, res=HT_ALL)
            break
        if stage == "mix":
            mixer(0, seg, 1)
            continue
        for l in range(depth):
            ffn(l, 0, 3 * l, True, (1 + 3 * l, False))
            layer_norm(1 + 3 * l, only=1)
            if stage == "ffn1":
                dump("xT", xT[:], [128, 8, T], res=XT_ALL)
                break
            mixer(l, seg, 3 * l + 1)
            layer_norm(2 + 3 * l)
            ffn(l, 1, 3 * l + 2, False, (3 + 3 * l, l == depth - 1))
            layer_norm(3 + 3 * l, final=(l == depth - 1), only=1)
        if stage != "full":
            break
        for k in range(8):
            S.op("sp", lambda e, k=k, seg=seg: e.dma_start(out=out_d[seg, :, k, :], in_=xT[:, k, :]),
                 reads=(("xT", k, 0), ("xT", k, 1)), writes=("outd",), dma="xout%d" % k)

    info = S.emit()
    return nc, es, info, dump_specs


def pack_inputs(inp, b):
    f32 = np.float32
    m = {}
    x = np.asarray(inp["x"][b], f32)
    m["xT"] = np.ascontiguousarray(x.reshape(NSEG, T, 8, 128).transpose(0, 3, 2, 1))
    m["c_r"] = _fm(inp["c"][b])
    m["consts"] = _consts()
    lnG = [inp["ln_in_g"]] + [inp["ln_g"][l, s] for l in range(DEPTH) for s in range(3)]
    lnB = [inp["ln_in_b"]] + [inp["ln_b"][l, s] for l in range(DEPTH) for s in range(3)]
    m["lnG"] = np.ascontiguousarray(np.concatenate([_fm(v) for v in lnG], axis=1))
    m["lnB"] = np.ascontiguousarray(np.concatenate([_fm(v) for v in lnB], axis=1))
    aw = np.asarray(inp["ada_w"], f32)
    m["adaw"] = np.ascontiguousarray(aw.reshape(DEPTH, 8, 128, 18, 512).transpose(0, 3, 2, 1, 4))
    m["adab"] = np.ascontiguousarray(np.asarray(inp["ada_b"], f32).reshape(DEPTH, 1, 9216))
    fwin = np.zeros((DEPTH, 2, NF, 128, 8, 256), f32)
    fwout = np.zeros((DEPTH, 2, 8, 128, NF, 128), f32)
    for wi, (kin, kout) in enumerate((("ffn1_w_in", "ffn1_w_out"), ("ffn2_w_in", "ffn2_w_out"))):
        for l in range(DEPTH):
            w = np.asarray(inp[kin][l], f32).reshape(8, 128, 2, NF, 128)
            fwin[l, wi] = w.transpose(3, 1, 0, 2, 4).reshape(NF, 128, 8, 256)
            wo = np.asarray(inp[kout][l], f32).reshape(NF, 128, 8, 128)
            fwout[l, wi] = wo.transpose(2, 1, 0, 3)
    m["fwin"], m["fwout"] = fwin, fwout
    mw = np.asarray(inp["mix_w_in"], f32)

    def cols(l, lo, hi):
        return _kmaj(mw[l][:, lo:hi])
    m["wxbc"] = np.stack([np.stack([cols(l, 512 + 128 * i, 512 + 128 * (i + 1)) for i in range(8)]) for l in range(DEPTH)])
    m["wqk"] = np.stack([np.stack([cols(l, 1544 + 64 * i, 1544 + 64 * (i + 1)) for i in range(8)]) for l in range(DEPTH)])
    m["wsc"] = np.stack([np.stack([cols(l, 2316 + 128 * i, 2316 + 128 * (i + 1)) for i in range(6)]) for l in range(DEPTH)])
    m["wf"] = np.stack([cols(l, 2312, 2316) for l in range(DEPTH)])
    m["wz"] = np.stack([cols(l, 0, 512) for l in range(DEPTH)])
    m["wdt"] = np.stack([cols(l, 1536, 1544) for l in range(DEPTH)])
    m["wv"] = np.stack([cols(l, 2056, 2312) for l in range(DEPTH)])
    wmo = np.asarray(inp["mix_w_out"], f32)
    m["wmo"] = np.ascontiguousarray(wmo.reshape(DEPTH, 8, 128, 8, 128).transpose(0, 3, 2, 1, 4))
    cwv = np.asarray(inp["ssd_conv_w"], f32)
    m["cw"] = np.ascontiguousarray(cwv.reshape(DEPTH, 4, 8, 128).transpose(0, 3, 2, 1).reshape(DEPTH, 128, 32))
    m["cb"] = np.stack([_fm(inp["ssd_conv_b"][l]) for l in range(DEPTH)])
    sp = np.concatenate([np.asarray(inp["ssd_dt_bias"], f32), np.asarray(inp["ssd_a_log"], f32),
                         np.asarray(inp["ssd_d"], f32)], axis=1)
    m["ssdp"] = np.ascontiguousarray(np.broadcast_to(sp[:, None, :], (DEPTH, 128, 24)))
    m["normg"] = np.ascontiguousarray(np.broadcast_to(np.asarray(inp["ssd_norm_g"], f32)[:, None, :], (DEPTH, 128, 512)))
    m["fb"] = np.ascontiguousarray(np.asarray(inp["fox_f_bias"], f32).reshape(DEPTH, 4, 1))
    scwv = np.asarray(inp["sconv_w"], f32)
    m["scw"] = np.ascontiguousarray(scwv.reshape(DEPTH, 3, 2, 128).transpose(0, 3, 2, 1).reshape(DEPTH, 128, 6))
    return {k: np.ascontiguousarray(v, dtype=f32) for k, v in m.items()}


_CACHE = {}


def kernel(**inputs):
    if "nc" not in _CACHE:
        _CACHE["nc"] = build("full")
    nc, es, info, _ = _CACHE["nc"]
    in_maps = [pack_inputs(inputs, b) for b in range(NCORES)]
    res = run_bass_kernel_spmd(nc, in_maps, core_ids=list(range(NCORES)))
    out = np.zeros((NCORES, SEQ, D), np.float32)
    for b in range(NCORES):
        o = res.results[b]["outT"]
        out[b] = o.transpose(0, 3, 2, 1).reshape(SEQ, D)
    return out
```
